# Optimizing a Trainium2 kernel written in Bass

```python
import math
import jax, jax.numpy as jnp
from jax import lax
import numpy as np

D_MODEL = 2048
BATCH = 8
SEQ = 2048
DEPTH = 1
DEC_BATCH = 2
DEC_SEQ = 8192
PAST_LEN = 128

ATT_HEADS = 16
ATT_HEAD_DIM = 64
ATT_WIDTH = ATT_HEADS * ATT_HEAD_DIM
DILATED_PAIRS = ((128, 1), (512, 4), (2048, 16))
N_BUCKETS = 32
MAX_DISTANCE = 1024
MLSTM_HEADS = 4
MLSTM_HEAD_DIM = 256
MLSTM_WIDTH = MLSTM_HEADS * MLSTM_HEAD_DIM
MLSTM_CHUNK = 64
CONV_K = 3
N_GATES = 4 * MLSTM_HEADS
IN_COLS = 3 * ATT_WIDTH + 4 * MLSTM_WIDTH + N_GATES
D_FF = 5632
PLE_DIM = 256
EPS = 1e-6
NEG = -1e30

kernel_name = 'hybrid_dilated_attn_mlstm_macaron_encoder'


def rms_norm(x, w):
    xf = x.astype(jnp.float32)
    y = xf * lax.rsqrt(jnp.mean(xf * xf, axis=-1, keepdims=True) + EPS)
    return (y * w.astype(jnp.float32)).astype(x.dtype)


def swiglu(x, w_in, w_out):
    g, u = jnp.split(x @ w_in, 2, axis=-1)
    return (jax.nn.silu(g) * u) @ w_out


def t5_bucket(rel):
    nb = N_BUCKETS // 2
    max_exact = nb // 2
    n = np.abs(rel)
    large = max_exact + (np.log(np.maximum(n, 1) / max_exact) / math.log(MAX_DISTANCE / max_exact) * (nb - max_exact)).astype(np.int32)
    large = np.minimum(large, nb - 1)
    return np.where(rel > 0, nb, 0) + np.where(n < max_exact, n, large)


def dilated_branch(q, k, v, rel_table, window, dilation):
    B, S, H, E = q.shape
    R = window // (2 * dilation)
    L = S // dilation
    nb = -(-L // R)
    Lp = nb * R

    def split(t):
        t = t.reshape(B, L, dilation, H, E).transpose(0, 2, 1, 3, 4)
        return jnp.pad(t, ((0, 0), (0, 0), (0, Lp - L), (0, 0), (0, 0)))

    def windows(t):
        t = jnp.pad(split(t), ((0, 0), (0, 0), (R, R), (0, 0), (0, 0))).reshape(B, dilation, nb + 2, R, H, E)
        return jnp.concatenate([t[:, :, :-2], t[:, :, 1:-1], t[:, :, 2:]], axis=3).astype(jnp.float32)

    qs = split(q).reshape(B, dilation, nb, R, H, E).astype(jnp.float32)
    kw, vw = windows(k), windows(v)
    off = np.arange(3 * R)[None, :] - R - np.arange(R)[:, None]
    band = np.abs(off) <= R
    kpos = (np.arange(nb)[:, None] - 1) * R + np.arange(3 * R)[None, :]
    mask = band[None] & ((kpos >= 0) & (kpos < L))[:, None, :]
    bias = jnp.transpose(rel_table[t5_bucket(off * dilation)], (2, 0, 1)).astype(jnp.float32)
    s = jnp.einsum('bdnqhe,bdnkhe->bdnhqk', qs, kw) * (E ** -0.5) + bias
    s = jnp.where(mask[:, None], s, NEG)
    mx = jnp.max(s, axis=-1)
    e = jnp.exp(s - mx[..., None])
    den = jnp.sum(e, axis=-1)
    num = jnp.einsum('bdnhqk,bdnkhe->bdnqhe', e, vw)

    def merge(t):
        tail = t.shape[4:]
        t = t.reshape(B, dilation, Lp, *tail)[:, :, :L]
        return jnp.swapaxes(t, 1, 2).reshape(B, S, *tail)

    return merge(num), merge(jnp.swapaxes(mx, -1, -2)), merge(jnp.swapaxes(den, -1, -2))


def dilated_attention(q, k, v, rel_table):
    B, S, _ = q.shape
    sh = lambda t: t.reshape(B, S, ATT_HEADS, ATT_HEAD_DIM)
    parts = [dilated_branch(sh(q), sh(k), sh(v), rel_table, w, d) for (w, d) in DILATED_PAIRS]
    m_all = parts[0][1]
    for _, m_b, _ in parts[1:]:
        m_all = jnp.maximum(m_all, m_b)
    num = 0.0
    den = 0.0
    for n_b, m_b, d_b in parts:
        scale = jnp.exp(m_b - m_all)
        num = num + n_b * scale[..., None]
        den = den + d_b * scale
    out = num / den[..., None]
    return out.reshape(B, S, ATT_WIDTH).astype(q.dtype)


def mlstm_scan(q, k, v, log_i, log_f):
    B, H, S, E = q.shape
    C = MLSTM_CHUNK
    nc = S // C

    def chunks(t):
        return jnp.moveaxis(t.reshape(B, H, nc, C, *t.shape[3:]), 2, 0)

    lower_tri = jnp.tril(jnp.ones((C, C), dtype=bool))

    def step(carry, xs):
        c_mat, n_vec, m = carry
        qc, kc, vc, li, lf = xs
        b = jnp.cumsum(lf, axis=-1)
        g = b[..., -1]
        d_mat = jnp.where(lower_tri, b[..., :, None] - b[..., None, :] + li[..., None, :], -jnp.inf)
        inter = b + m[..., None]
        m_t = jnp.maximum(inter, jnp.max(d_mat, axis=-1))
        w_intra = jnp.exp(d_mat - m_t[..., None]) * jnp.einsum('bhtk,bhsk->bhts', qc, kc)
        w_inter = jnp.exp(inter - m_t)
        num = jnp.einsum('bhts,bhsv->bhtv', w_intra, vc) + w_inter[..., None] * jnp.einsum('bhvk,bhtk->bhtv', c_mat, qc)
        den = jnp.sum(w_intra, axis=-1) + w_inter * jnp.einsum('bhk,bhtk->bht', n_vec, qc)
        h = num / jnp.maximum(jnp.abs(den), jnp.exp(-m_t))[..., None]
        a = g[..., None] - b + li
        m_new = jnp.maximum(g + m, jnp.max(a, axis=-1))
        w_a = jnp.exp(a - m_new[..., None])
        decay = jnp.exp(g + m - m_new)
        c_new = decay[..., None, None] * c_mat + jnp.einsum('bhsv,bhsk->bhvk', vc * w_a[..., None], kc)
        n_new = decay[..., None] * n_vec + jnp.einsum('bhs,bhsk->bhk', w_a, kc)
        return (c_new, n_new, m_new), h

    init = (jnp.zeros((B, H, E, E), jnp.float32), jnp.zeros((B, H, E), jnp.float32),
            jnp.full((B, H), NEG, jnp.float32))
    _, h = lax.scan(step, init, (chunks(q), chunks(k), chunks(v), chunks(log_i), chunks(log_f)))
    return jnp.moveaxis(h, 0, 2).reshape(B, H, S, E)


def mlstm_mixer(q, k, v, o_pre, gates, norm_w):
    B, S, _ = q.shape
    heads = lambda t: t.reshape(B, S, MLSTM_HEADS, MLSTM_HEAD_DIM).transpose(0, 2, 1, 3).astype(jnp.float32)
    qh, kh, vh = heads(q), heads(k) * (MLSTM_HEAD_DIM ** -0.5), heads(v)
    i_f, f_f, i_b, f_b = gates.astype(jnp.float32).reshape(B, S, 4, MLSTM_HEADS).transpose(2, 0, 3, 1)
    h_fwd = mlstm_scan(qh, kh, vh, i_f, jax.nn.log_sigmoid(f_f))
    flip = lambda t: jnp.flip(t, axis=2)
    h_bwd = flip(mlstm_scan(flip(qh), flip(kh), flip(vh), flip(i_b), jax.nn.log_sigmoid(flip(f_b))))
    h = h_fwd + h_bwd
    mu = jnp.mean(h, axis=-1, keepdims=True)
    var = jnp.mean(jnp.square(h - mu), axis=-1, keepdims=True)
    h = ((h - mu) * lax.rsqrt(var + EPS)).transpose(0, 2, 1, 3).reshape(B, S, MLSTM_WIDTH)
    h = h * norm_w.astype(jnp.float32) * jax.nn.sigmoid(o_pre.astype(jnp.float32))
    return h.astype(q.dtype)


def centred_dwconv(x, w, b):
    K = w.shape[0]
    y = lax.conv_general_dilated(x, w[:, None, :].astype(x.dtype), window_strides=(1,),
                                 padding=[(K // 2, K // 2)], dimension_numbers=('NWC', 'WIO', 'NWC'),
                                 feature_group_count=x.shape[-1])
    return y + b.astype(x.dtype)


def token_mixing(h, rel_table, w_in, b_gates, conv_w, conv_b, attn_out_norm, mlstm_out_norm, w_out):
    A, M = ATT_WIDTH, MLSTM_WIDTH
    z = h @ w_in
    aq, ak, av = z[..., :A], z[..., A:2 * A], z[..., 2 * A:3 * A]
    qk_m = jax.nn.silu(centred_dwconv(z[..., 3 * A:3 * A + 2 * M], conv_w, conv_b))
    mq, mk = qk_m[..., :M], qk_m[..., M:]
    mv = z[..., 3 * A + 2 * M:3 * A + 3 * M]
    mo = z[..., 3 * A + 3 * M:3 * A + 4 * M]
    gates = z[..., 3 * A + 4 * M:] + b_gates.astype(z.dtype)
    y_att = rms_norm(dilated_attention(aq, ak, av, rel_table), attn_out_norm)
    y_mem = mlstm_mixer(mq, mk, mv, mo, gates, mlstm_out_norm)
    return jnp.concatenate([y_att, y_mem], axis=-1) @ w_out


def encoder_trunk(x, pe, rel_table, ffn1_norm, ffn1_w_in, ffn1_w_out, mix_norm, w_in, b_gates,
                  conv_w, conv_b, attn_out_norm, mlstm_out_norm, w_out, ffn2_norm, ffn2_w_in,
                  ffn2_w_out, ple_norm, ple_w_gate, ple_w_proj, final_norm):
    for i in range(DEPTH):
        x = x + 0.5 * swiglu(rms_norm(x, ffn1_norm[i]), ffn1_w_in[i], ffn1_w_out[i])
        x = x + token_mixing(rms_norm(x, mix_norm[i]), rel_table, w_in[i], b_gates[i], conv_w[i], conv_b[i],
                             attn_out_norm[i], mlstm_out_norm[i], w_out[i])
        x = x + 0.5 * swiglu(rms_norm(x, ffn2_norm[i]), ffn2_w_in[i], ffn2_w_out[i])
        gate = jax.nn.sigmoid(rms_norm(x, ple_norm[i]) @ ple_w_gate[i])
        x = x + gate * (pe[i] @ ple_w_proj[i])
    return rms_norm(x, final_norm)


def setup_inputs(seed: int = 0) -> dict:
    key = jax.random.key(seed)
    ks = jax.random.split(key, 24)
    f32 = jnp.float32
    nrm = lambda k, shape, s: jax.random.normal(k, shape, f32) * s
    gain = lambda k, shape: 1.0 + 0.01 * jax.random.normal(k, shape, f32)
    i_bias = nrm(ks[6], (DEPTH, MLSTM_HEADS), 0.1)
    f_bias = jnp.linspace(3.0, 6.0, MLSTM_HEADS, dtype=f32)[None] + nrm(ks[7], (DEPTH, MLSTM_HEADS), 0.01)
    i_bias_b = nrm(ks[8], (DEPTH, MLSTM_HEADS), 0.1)
    f_bias_b = jnp.linspace(3.0, 6.0, MLSTM_HEADS, dtype=f32)[None] + nrm(ks[9], (DEPTH, MLSTM_HEADS), 0.01)
    return {
        'x_prompt': nrm(ks[0], (BATCH, SEQ, D_MODEL), 1.0),
        'x_sample': nrm(ks[1], (DEC_BATCH, DEC_SEQ, D_MODEL), 1.0),
        'p_prompt': nrm(ks[2], (DEPTH, BATCH, SEQ, PLE_DIM), 1.0),
        'p_sample': nrm(ks[3], (DEPTH, DEC_BATCH, DEC_SEQ, PLE_DIM), 1.0),
        'rel_table': nrm(ks[4], (N_BUCKETS, ATT_HEADS), 0.2),
        'ffn1_norm': gain(ks[5], (DEPTH, D_MODEL)),
        'ffn1_w_in': nrm(ks[10], (DEPTH, D_MODEL, 2 * D_FF), D_MODEL ** -0.5),
        'ffn1_w_out': nrm(ks[11], (DEPTH, D_FF, D_MODEL), D_FF ** -0.5),
        'mix_norm': gain(ks[12], (DEPTH, D_MODEL)),
        'w_in': nrm(ks[13], (DEPTH, D_MODEL, IN_COLS), D_MODEL ** -0.5),
        'b_gates': jnp.concatenate([i_bias, f_bias, i_bias_b, f_bias_b], axis=-1),
        'conv_w': nrm(ks[14], (DEPTH, CONV_K, 2 * MLSTM_WIDTH), CONV_K ** -0.5),
        'conv_b': nrm(ks[15], (DEPTH, 2 * MLSTM_WIDTH), 0.01),
        'attn_out_norm': gain(ks[16], (DEPTH, ATT_WIDTH)),
        'mlstm_out_norm': gain(ks[17], (DEPTH, MLSTM_WIDTH)),
        'w_out': nrm(ks[18], (DEPTH, D_MODEL, D_MODEL), D_MODEL ** -0.5),
        'ffn2_norm': gain(ks[19], (DEPTH, D_MODEL)),
        'ffn2_w_in': nrm(ks[20], (DEPTH, D_MODEL, 2 * D_FF), D_MODEL ** -0.5),
        'ffn2_w_out': nrm(ks[21], (DEPTH, D_FF, D_MODEL), D_FF ** -0.5),
        'ple_norm': gain(ks[22], (DEPTH, D_MODEL)),
        'ple_w_gate': nrm(ks[23], (DEPTH, D_MODEL, D_MODEL), D_MODEL ** -0.5),
        'ple_w_proj': nrm(jax.random.fold_in(key, 101), (DEPTH, PLE_DIM, D_MODEL), PLE_DIM ** -0.5),
        'final_norm': gain(jax.random.fold_in(key, 102), (D_MODEL,)),
    }


def reference(x_prompt, x_sample, p_prompt, p_sample, rel_table, ffn1_norm, ffn1_w_in, ffn1_w_out,
              mix_norm, w_in, b_gates, conv_w, conv_b, attn_out_norm, mlstm_out_norm, w_out,
              ffn2_norm, ffn2_w_in, ffn2_w_out, ple_norm, ple_w_gate, ple_w_proj, final_norm):
    y_prompt = encoder_trunk(x_prompt, p_prompt, rel_table, ffn1_norm, ffn1_w_in, ffn1_w_out, mix_norm, w_in,
                             b_gates, conv_w, conv_b, attn_out_norm, mlstm_out_norm, w_out, ffn2_norm,
                             ffn2_w_in, ffn2_w_out, ple_norm, ple_w_gate, ple_w_proj, final_norm)
    y_sample = encoder_trunk(x_sample, p_sample, rel_table, ffn1_norm, ffn1_w_in, ffn1_w_out, mix_norm, w_in,
                             b_gates, conv_w, conv_b, attn_out_norm, mlstm_out_norm, w_out, ffn2_norm,
                             ffn2_w_in, ffn2_w_out, ple_norm, ple_w_gate, ple_w_proj, final_norm)
    return (y_prompt, y_sample)
```

```python
import numpy as np
import concourse.bass as bass
import concourse.mybir as mybir
from concourse.bass_utils import run_bass_kernel_spmd

F32 = mybir.dt.float32
BF16 = mybir.dt.bfloat16
AF = mybir.ActivationFunctionType
ALU = mybir.AluOpType
AX = mybir.AxisListType

D = 2048
KC = 16
DFF = 5632
NJ = 44
AW = 1024
MW = 1024
NG = 16
INC = 3 * AW + 4 * MW + NG
PLE = 256
EPS = 1e-6
NT = 512
SEG = 2048
HALO = 1024


class Buf:
    __slots__ = ("name", "w", "r")

    def __init__(self, name):
        self.name = name
        self.w = None
        self.r = {}


class Prog:
    NDSLOT = 6

    def __init__(self, nc, stack):
        self.nc = nc
        self.stack = stack
        self.handles = {"pe": nc.tensor, "act": nc.scalar, "dve": nc.vector, "pool": nc.gpsimd, "sp": nc.sync}
        self.ops = {k: [] for k in self.handles}
        self.cnt = {k: 0 for k in self.handles}
        self.known = {k: {} for k in self.handles}
        self.sem = {k: stack.enter_context(nc.semaphore("sem_" + k)) for k in self.handles}
        self.dsem = {}
        self.dcnt = {}
        for q in ("sp", "pool", "act"):
            self.dsem[q] = [stack.enter_context(nc.semaphore("dsem_%s_%d" % (q, i))) for i in range(self.NDSLOT)]
            self.dcnt[q] = 0
        self.all_events = {}

    def _need(self, eng, dep):
        if dep is None:
            return
        sem, val, key = dep
        if eng == "pe" and key == "pe":
            return
        if self.known[eng].get(key, 0) >= val:
            return
        self.known[eng][key] = val
        self.ops[eng].append(("wait", sem, val))

    def _deps(self, eng, reads, writes):
        for b in reads:
            self._need(eng, b.w)
        for b in writes:
            self._need(eng, b.w)
            for d in list(b.r.values()):
                self._need(eng, d)

    def op(self, eng, fn, reads=(), writes=()):
        self._deps(eng, reads, writes)
        self.cnt[eng] += 1
        ev = (self.sem[eng], self.cnt[eng], eng)
        self.ops[eng].append(("op", fn, self.sem[eng], 1))
        self.all_events[eng] = ev
        for b in reads:
            b.r[eng] = ev
        for b in writes:
            b.w = ev
            b.r = {}
        return ev

    def dma(self, q, out, in_, reads=(), writes=(), **kw):
        k = self.dcnt[q]
        slot = k % self.NDSLOT
        sem = self.dsem[q][slot]
        key = "d_%s_%d" % (q, slot)
        if k >= self.NDSLOT:
            self._need(q, (sem, 16 * (k // self.NDSLOT), key))
        self._deps(q, reads, writes)
        self.dcnt[q] += 1
        ev = (sem, 16 * (k // self.NDSLOT + 1), key)
        self.ops[q].append(("op", lambda h: h.dma_start(out=out, in_=in_, **kw), sem, 16))
        self.all_events[key] = ev
        for b in reads:
            b.r[key] = ev
        for b in writes:
            b.w = ev
            b.r = {}
        return ev

    def barrier(self, engines=None):
        evs = list(self.all_events.values())
        for e in (engines or self.handles):
            for ev in evs:
                self._need(e, ev)

    def emit(self):
        nc = self.nc
        with nc.Block() as block:
            def run(name):
                def body(h):
                    for o in self.ops[name]:
                        if o[0] == "wait":
                            h.wait_ge(o[1], o[2])
                        else:
                            o[1](h).then_inc(o[2], o[3])
                return body
            block.tensor(run("pe"))
            block.scalar(run("act"))
            block.vector(run("dve"))
            block.gpsimd(run("pool"))
            block.sync(run("sp"))


def _lay_w_in_blocks(w, nblk_cols):
    K, N = w.shape
    kc = K // 128
    nb = N // nblk_cols
    return np.ascontiguousarray(w.reshape(kc, 128, nb, nblk_cols).transpose(2, 1, 0, 3)).reshape(nb, 128, kc * nblk_cols)


def _lay_ffn_in(w):
    t = w.reshape(KC, 128, 2, NJ // 2, 2, 128)
    t = t.transpose(3, 1, 4, 0, 2, 5)
    return np.ascontiguousarray(t).reshape(NJ // 2, 128, 2 * KC * 2 * 128)


def _lay_ffn_out(w):
    t = w.reshape(NJ, 128, KC, 128).transpose(2, 1, 0, 3)
    return np.ascontiguousarray(t).reshape(KC, 128, NJ * 128)


def _lay_sq(w, kc_in):
    t = w.reshape(kc_in, 128, 4, 4, 128).transpose(2, 1, 3, 0, 4)
    return np.ascontiguousarray(t).reshape(4, 128, 4 * kc_in * 128)


def _lay_vec(v):
    return np.ascontiguousarray(v.reshape(-1, 128).T)


DIL = (1, 4, 16)
import os as _os
ATT_PAIRS = int(_os.environ.get('ATT_PAIRS', '8'))
ATT_DILS = tuple(int(x) for x in _os.environ.get('ATT_DILS', '1,4,16').split(','))
NEGBIG = -1.0e30


def build(T, debug=False, phases=(1, 2, 3, 4, 5)):
    from contextlib import ExitStack
    NTILE = T // NT
    NSEG = T // SEG
    NCH = T // 128
    NQ = NCH * 8
    NB = NQ // 128
    nc = bass.Bass("TRN2", target_bir_lowering=False)
    st = ExitStack()
    P = Prog(nc, st)

    def din(name, shape, dt=F32):
        return nc.dram_tensor(name, list(shape), dt, kind="ExternalInput").ap()

    def dscr(name, shape, dt):
        kind = "ExternalOutput" if debug else "Internal"
        return nc.dram_tensor(name, list(shape), dt, kind=kind).ap()

    xT = din("xT", [D, T])
    pT = din("pT", [PLE, T])
    w1in = din("w1in", [NJ // 2, 128, 8192]); w1out = din("w1out", [KC, 128, NJ * 128])
    w2in = din("w2in", [NJ // 2, 128, 8192]); w2out = din("w2out", [KC, 128, NJ * 128])
    wina = din("wina", [8, 128, 8192])
    winb = din("winb", [6, 128, 8192])
    wing = din("wing", [128, KC * NG])
    wout = din("wout", [4, 128, 8192])
    wgate = din("wgate", [4, 128, 8192])
    wproj = din("wproj", [4, 128, 4 * 2 * 128])
    nrm = din("nrm", [128, 5 * KC])
    anrm = din("anrm", [128, 8])
    mnrm = din("mnrm", [128, MW])
    bgr = din("bgr", [128, NG])
    cvw = din("cvw", [128, 16, 3])
    cvb = din("cvb", [128, 16])
    sepi = din("sep", [128, 1])
    relb = din("relb", [16, 128, 6, 128])
    cmask = din("cmask", [128, 16, 128])
    cf32 = din("cf32", [128, 3, 128])
    perm = din("perm", [NB, 128, NQ])
    segm = din("segm", [128, NCH])
    yT = nc.dram_tensor("yT", [D, T], F32, kind="ExternalOutput").ap()
    s_w1in = dscr("s_w1in", [NJ // 2, 128, 8192], BF16); s_w1out = dscr("s_w1out", [KC, 128, NJ * 128], BF16)
    s_w2in = dscr("s_w2in", [NJ // 2, 128, 8192], BF16); s_w2out = dscr("s_w2out", [KC, 128, NJ * 128], BF16)
    s_wina = dscr("s_wina", [8, 128, 8192], BF16); s_winb = dscr("s_winb", [6, 128, 8192], BF16)
    s_wing = dscr("s_wing", [128, KC * NG], BF16)
    s_wout = dscr("s_wout", [4, 128, 8192], BF16); s_wgate = dscr("s_wgate", [4, 128, 8192], BF16)
    s_wproj = dscr("s_wproj", [4, 128, 1024], BF16)
    s_pT = dscr("s_pT", [PLE, T], BF16)
    s_x1 = dscr("s_x1", [D, T], F32)
    s_zfm = dscr("s_zfm", [4096, HALO + T + HALO], BF16)
    s_ztm = dscr("s_ztm", [HALO + T + HALO, 3072], BF16)
    s_yT = dscr("s_yT", [D, T], BF16)
    s_hf = dscr("s_hf", [T, 256], F32)

    wb = {}
    def cast_blocks(name, src, dst, nblk):
        for b in range(nblk):
            bf = Buf("%s_%d" % (name, b))
            P.dma("pool", dst[b], src[b], writes=[bf])
            wb[(name, b)] = bf
    bf = Buf("wing"); P.dma("pool", s_wing, wing, writes=[bf]); wb[("wing", 0)] = bf
    cast_blocks("w1in", w1in, s_w1in, NJ // 2)
    cast_blocks("w1out", w1out, s_w1out, KC)
    cast_blocks("wina", wina, s_wina, 8)
    cast_blocks("winb", winb, s_winb, 6)
    b_pT = Buf("s_pT")
    def deferred_casts():
        cast_blocks("wout", wout, s_wout, 4)
        cast_blocks("w2in", w2in, s_w2in, NJ // 2)
        cast_blocks("w2out", w2out, s_w2out, KC)
        cast_blocks("wgate", wgate, s_wgate, 4)
        cast_blocks("wproj", wproj, s_wproj, 4)
        P.dma("pool", s_pT, pT, writes=[b_pT])

    if _os.environ.get('BARRIER0'):
        P.barrier()
    def sb(name, shape, dt):
        return st.enter_context(nc.sbuf_tensor(name, list(shape), dt))
    b_c = Buf("consts")
    ones_bf = sb("ones_bf", [128, 128], BF16)
    P.op("dve", lambda h: h.memset(ones_bf[:, :], 1.0), writes=[b_c])
    ones_f = sb("ones_f", [128, 128], F32)
    P.op("dve", lambda h: h.memset(ones_f[:, :], 1.0), writes=[b_c])
    zeros_bf = sb("zeros_bf", [128, 3072], BF16)
    P.op("dve", lambda h: h.memset(zeros_bf[:, :], 0.0), writes=[b_c])
    eps_sb = sb("eps_sb", [128, 1], F32)
    P.op("dve", lambda h: h.memset(eps_sb[:, :], EPS), writes=[b_c])
    nrm_sb = sb("nrm_sb", [128, 5 * KC], F32); P.dma("sp", nrm_sb[:, :], nrm[:, :], writes=[b_c])
    anrm_sb = sb("anrm_sb", [128, 8], F32); P.dma("sp", anrm_sb[:, :], anrm[:, :], writes=[b_c])
    bgr_sb = sb("bgr_sb", [128, NG], F32); P.dma("sp", bgr_sb[:, :], bgr[:, :], writes=[b_c])
    cvw_sb = sb("cvw_sb", [128, 16, 3], F32); P.dma("sp", cvw_sb[:, :, :], cvw[:, :, :], writes=[b_c])
    cvb_sb = sb("cvb_sb", [128, 16], F32); P.dma("sp", cvb_sb[:, :], cvb[:, :], writes=[b_c])
    sep_sb = sb("sep_sb", [128, 1], F32); P.dma("sp", sep_sb[:, :], sepi[:, :], writes=[b_c])
    cf_sb = sb("cf_sb", [128, 3, 128], F32); P.dma("sp", cf_sb[:, :, :], cf32[:, :, :], writes=[b_c])
    cm_sb = sb("cm_sb", [128, 16, 128], F32); P.dma("sp", cm_sb[:, :, :], cmask[:, :, :], writes=[b_c])
    ident_bf = sb("ident_bf", [128, 128], BF16)
    P.op("dve", lambda h: h.tensor_copy(out=ident_bf[:, :], in_=cf_sb[:, 2, :]), reads=[b_c], writes=[b_c])
    cmc_bf = sb("cmc_bf", [128, 2, 128], BF16)
    P.op("dve", lambda h: h.tensor_copy(out=cmc_bf[:, :, :], in_=cm_sb[:, 12:14, :]), reads=[b_c], writes=[b_c])
    sepneg = sb("sepneg", [128, 1], F32)
    P.op("dve", lambda h: h.tensor_scalar(out=sepneg[:, :], in0=sep_sb[:, :], scalar1=NEGBIG, scalar2=None, op0=ALU.mult), reads=[b_c], writes=[b_c])
    w0s = sb("w0s", [128, 16], F32); w2s = sb("w2s", [128, 16], F32)
    P.op("dve", lambda h: h.tensor_scalar(out=w0s[:, :], in0=cvw_sb[:, :, 0], scalar1=sep_sb[:, 0:1], scalar2=None, op0=ALU.mult), reads=[b_c], writes=[b_c])
    P.op("dve", lambda h: h.tensor_scalar(out=w2s[:, :], in0=cvw_sb[:, :, 2], scalar1=sep_sb[:, 0:1], scalar2=None, op0=ALU.mult), reads=[b_c], writes=[b_c])
    wing_sb = sb("wing_sb", [128, KC * NG], BF16)
    P.dma("sp", wing_sb[:, :], s_wing, reads=[wb[("wing", 0)]], writes=[b_c])
    Gtm = sb("Gtm", [128, NCH, NG], F32); b_G = Buf("Gtm")
    Ug = sb("Ug", [128, NCH, 8], F32); Fg = sb("Fg", [128, NCH, 8], F32)
    NEGMC = sb("NEGMC", [128, 8, NCH], F32); DEC = sb("DEC", [128, 8, NCH], F32)
    b_scan = Buf("scan")

    b_pad = Buf("pads")
    pad_bufs = []
    zfm_v = s_zfm.rearrange("(c p) t -> p c t", p=128)
    padj = sb("padj", [128, 1], F32)
    def pad_dma(out, in_):
        pb = Buf("pad%d" % len(pad_bufs)); pad_bufs.append(pb)
        P.dma("sp", out, in_, reads=[b_c], writes=[pb])
    def emit_pads():
        for c in range(8, 32):
            pad_dma(zfm_v[:, c, 0:HALO], zeros_bf[:, 0:HALO])
            pad_dma(zfm_v[:, c, HALO + T:HALO + T + HALO], zeros_bf[:, 0:HALO])
        for r in range(HALO // 128):
            pad_dma(s_ztm[r * 128:(r + 1) * 128, :], zeros_bf[:, :])
            pad_dma(s_ztm[HALO + T + r * 128:HALO + T + (r + 1) * 128, :], zeros_bf[:, :])
        P.op("pool", lambda h: h.memset(padj[:, :], 0.0), reads=pad_bufs, writes=[b_pad])

    NPS = 6
    ps = [st.enter_context(nc.psum_tensor("ps%d" % i, [128, 512], F32)) for i in range(NPS)]
    b_ps = [Buf("ps%d" % i) for i in range(NPS)]
    psT = [st.enter_context(nc.psum_tensor("psT%d" % i, [128, 8, 128], BF16)) for i in range(2)]
    b_psT = [Buf("psT%d" % i) for i in range(2)]
    ps_rr = [0]; pst_rr = [0]
    ps_reserved = set()
    def next_ps():
        while True:
            i = ps_rr[0] % NPS
            ps_rr[0] += 1
            if i not in ps_reserved:
                return ps[i], b_ps[i]
    def next_pst():
        i = pst_rr[0] % 2
        pst_rr[0] += 1
        return psT[i], b_psT[i]

    xT_v = xT.rearrange("(kc p) t -> p kc t", p=128)
    x1_v = s_x1.rearrange("(kc p) t -> p kc t", p=128)
    yT_v = yT.rearrange("(kc p) t -> p kc t", p=128)
    syT_v = s_yT.rearrange("(kc p) t -> p kc t", p=128)
    spT_v = s_pT.rearrange("(kc p) t -> p kc t", p=128)
    b_x1 = [Buf("x1_%d" % i) for i in range(NTILE)]
    b_z = Buf("z")
    b_yT = Buf("s_yT")
    gen = [0]

    def dense_phase(which):
        dense = ExitStack()
        gen[0] += 1
        def dsb(name, shape, dt):
            return dense.enter_context(nc.sbuf_tensor("%s_g%d" % (name, gen[0]), list(shape), dt))
        X = dsb("X", [128, KC, NT], F32); b_X = Buf("X")
        HN = dsb("HN", [128, KC, NT], BF16); b_HN = Buf("HN"); b_HNm = [Buf("HNm%d" % m) for m in range(KC)]
        A = dsb("A", [128, NJ, NT], BF16); b_A = [Buf("A%d" % j) for j in range(NJ)]
        NWS = 3
        WS = [dsb("WS%d" % i, [128, 8192], BF16) for i in range(NWS)]; b_WS = [Buf("WS%d" % i) for i in range(NWS)]
        ws_rr = [0]
        RSTD = dsb("RSTD", [128, NT], F32); b_RSTD = Buf("RSTD")
        TMP = [dsb("TMP%d" % i, [128, NT], F32) for i in range(3)]; b_TMP = [Buf("TMP%d" % i) for i in range(3)]
        tmp_rr = [0]
        STG = [dsb("STG%d" % i, [128, NT], BF16) for i in range(6)]; b_STG = [Buf("STG%d" % i) for i in range(6)]
        stg_rr = [0]
        PTs = dsb("PTs", [128, 2, NT], BF16); b_PT = Buf("PT")
        YTs = dsb("YTs", [128, KC, NT], BF16); b_YT = Buf("YT")

        def load_w(src_ap, srcbuf, nelem=8192):
            i = ws_rr[0] % NWS
            ws_rr[0] += 1
            P.dma("sp", WS[i][:, 0:nelem], src_ap, reads=[srcbuf], writes=[b_WS[i]])
            return WS[i], b_WS[i]
        def next_tmp():
            i = tmp_rr[0] % 3
            tmp_rr[0] += 1
            return TMP[i], b_TMP[i]
        def next_stg():
            i = stg_rr[0] % 6
            stg_rr[0] += 1
            return STG[i], b_STG[i]

        def rms_stats():
            P.op("act", lambda h: h.activation(out=HN[:, :, :], in_=X[:, :, :], func=AF.Square), reads=[b_X], writes=[b_HN])
            pt, bp = next_ps()
            for kc in range(KC):
                P.op("pe", lambda h, kc=kc: h.matmul(pt[:, :], ones_bf[:, :], HN[:, kc, :], start=(kc == 0), stop=(kc == KC - 1)),
                     reads=[b_HN, b_c], writes=[bp])
            tt, bt = next_tmp()
            P.op("act", lambda h: h.activation(out=tt[:, :], in_=pt[:, :], func=AF.Sqrt, bias=eps_sb[:, 0:1], scale=1.0 / D),
                 reads=[bp, b_c], writes=[bt])
            P.op("dve", lambda h: h.reciprocal(out=RSTD[:, :], in_=tt[:, :]), reads=[bt], writes=[b_RSTD])

        stats = {}
        def fused_sq(m):
            if m == 0:
                stats["pt"], stats["bp"] = next_ps()
                ps_reserved.add(ps.index(stats["pt"]))
            P.op("act", lambda h, m=m: h.activation(out=HN[:, m, :], in_=X[:, m, :], func=AF.Square), reads=[b_X], writes=[b_HNm[m], b_HN])
            stats["pend"] = m
        def flush_sq():
            m = stats.get("pend")
            if m is None:
                return
            stats["pend"] = None
            pt, bp = stats["pt"], stats["bp"]
            P.op("pe", lambda h, m=m, pt=pt: h.matmul(pt[:, :], ones_bf[:, :], HN[:, m, :], start=(m == 0), stop=(m == KC - 1)),
                 reads=[b_HNm[m], b_c], writes=[bp])
        def rms_finish_stats():
            flush_sq()
            pt, bp = stats["pt"], stats["bp"]
            tt, bt = next_tmp()
            P.op("act", lambda h: h.activation(out=tt[:, :], in_=pt[:, :], func=AF.Sqrt, bias=eps_sb[:, 0:1], scale=1.0 / D),
                 reads=[bp, b_c], writes=[bt])
            ps_reserved.discard(ps.index(pt))
            P.op("dve", lambda h: h.reciprocal(out=RSTD[:, :], in_=tt[:, :]), reads=[bt], writes=[b_RSTD])

        def rmsnorm_to_HN(norm_idx, fused=False):
            if fused:
                rms_finish_stats()
            else:
                rms_stats()
            for kc in range(KC):
                gcol = nrm_sb[:, norm_idx * KC + kc: norm_idx * KC + kc + 1]
                P.op("dve", lambda h, kc=kc, gcol=gcol: h.scalar_tensor_tensor(
                    out=HN[:, kc, :], in0=X[:, kc, :], scalar=gcol, in1=RSTD[:, :], op0=ALU.mult, op1=ALU.mult),
                    reads=[b_X, b_RSTD, b_c], writes=[b_HN, b_HNm[kc]])

        def ffn(win_s, wout_s, win_name, wout_name):
            for jj in range(NJ // 2):
                W, bW = load_w(win_s[jj], wb[(win_name, jj)])
                Wv = W[:, :].rearrange("p (jl kc gu c) -> p jl kc gu c", jl=2, kc=KC, gu=2)
                def evac(j, pg, bpg, pu, bpu):
                    tt, bt = next_tmp()
                    P.op("act", lambda h, tt=tt, pg=pg: h.activation(out=tt[:, :], in_=pg[:, :], func=AF.Silu), reads=[bpg], writes=[bt])
                    P.op("dve", lambda h, tt=tt, pu=pu, j=j: h.tensor_tensor(out=A[:, j, :], in0=tt[:, :], in1=pu[:, :], op=ALU.mult),
                         reads=[bt, bpu], writes=[b_A[j]])
                if jj == 0:
                    banks = [(next_ps(), next_ps()) for jl in range(2)]
                    for kc in range(KC):
                        for jl in range(2):
                            (pg, bpg), (pu, bpu) = banks[jl]
                            P.op("pe", lambda h, kc=kc, jl=jl, pg=pg, Wv=Wv: h.matmul(pg[:, :], Wv[:, jl, kc, 0, :], HN[:, kc, :],
                                 start=(kc == 0), stop=(kc == KC - 1)), reads=[bW, b_HNm[kc]], writes=[bpg])
                            P.op("pe", lambda h, kc=kc, jl=jl, pu=pu, Wv=Wv: h.matmul(pu[:, :], Wv[:, jl, kc, 1, :], HN[:, kc, :],
                                 start=(kc == 0), stop=(kc == KC - 1)), reads=[bW, b_HNm[kc]], writes=[bpu])
                    for jl in range(2):
                        (pg, bpg), (pu, bpu) = banks[jl]
                        evac(jj * 2 + jl, pg, bpg, pu, bpu)
                else:
                    for jl in range(2):
                        pg, bpg = next_ps()
                        pu, bpu = next_ps()
                        for kc in range(KC):
                            P.op("pe", lambda h, kc=kc, jl=jl, pg=pg, Wv=Wv: h.matmul(pg[:, :], Wv[:, jl, kc, 0, :], HN[:, kc, :],
                                 start=(kc == 0), stop=(kc == KC - 1)), reads=[bW, b_HN], writes=[bpg])
                        for kc in range(KC):
                            P.op("pe", lambda h, kc=kc, jl=jl, pu=pu, Wv=Wv: h.matmul(pu[:, :], Wv[:, jl, kc, 1, :], HN[:, kc, :],
                                 start=(kc == 0), stop=(kc == KC - 1)), reads=[bW, b_HN], writes=[bpu])
                        evac(jj * 2 + jl, pg, bpg, pu, bpu)
            for m in range(KC):
                W, bW = load_w(wout_s[m][:, 0:NJ * 128], wb[(wout_name, m)], nelem=NJ * 128)
                Wv = W[:, 0:NJ * 128].rearrange("p (j c) -> p j c", j=NJ)
                po, bpo = next_ps()
                for j in range(NJ):
                    P.op("pe", lambda h, j=j, po=po, Wv=Wv: h.matmul(po[:, :], Wv[:, j, :], A[:, j, :], start=(j == 0), stop=(j == NJ - 1)),
                         reads=[bW, b_A[j]], writes=[bpo])
                flush_sq()
                P.op("dve", lambda h, m=m, po=po: h.scalar_tensor_tensor(out=X[:, m, :], in0=po[:, :], scalar=0.5, in1=X[:, m, :],
                     op0=ALU.mult, op1=ALU.add), reads=[bpo, b_X], writes=[b_X])
                fused_sq(m)

        if which == 1:
            P.dma("sp", X[:, :, :], xT_v[:, :, 0:NT], writes=[b_X])
            for it in range(NTILE):
                t0 = it * NT
                rmsnorm_to_HN(0)
                ffn(s_w1in, s_w1out, "w1in", "w1out")
                P.dma("sp", x1_v[:, :, t0:t0 + NT], X[:, :, :], reads=[b_X], writes=[b_x1[it]])
                rmsnorm_to_HN(1, fused=True)
                if it + 1 < NTILE:
                    P.dma("sp", X[:, :, :], xT_v[:, :, t0 + NT:t0 + 2 * NT], writes=[b_X])
                for cb in range(8):
                    W, bW = load_w(s_wina[cb], wb[("wina", cb)])
                    Wv = W[:, :].rearrange("p (cl kc c) -> p cl kc c", cl=4, kc=KC)
                    def zevac(c, pz, bpz):
                        sg, bsg = next_stg()
                        P.op("act", lambda h, sg=sg, pz=pz: h.copy(out=sg[:, :], in_=pz[:, :]), reads=[bpz], writes=[bsg])
                        P.dma("act", zfm_v[:, c, HALO + t0:HALO + t0 + NT], sg[:, :], reads=[bsg], writes=[Buf('zst')])
                    if cb == 0:
                        zb = [next_ps() for cl in range(4)]
                        for kc in range(KC):
                            for cl in range(4):
                                pz, bpz = zb[cl]
                                P.op("pe", lambda h, kc=kc, cl=cl, pz=pz, Wv=Wv: h.matmul(pz[:, :], Wv[:, cl, kc, :], HN[:, kc, :],
                                     start=(kc == 0), stop=(kc == KC - 1)), reads=[bW, b_HNm[kc]], writes=[bpz])
                        for cl in range(4):
                            zevac(cb * 4 + cl, *zb[cl])
                    else:
                        for cl in range(4):
                            pz, bpz = next_ps()
                            for kc in range(KC):
                                P.op("pe", lambda h, kc=kc, cl=cl, pz=pz, Wv=Wv: h.matmul(pz[:, :], Wv[:, cl, kc, :], HN[:, kc, :],
                                     start=(kc == 0), stop=(kc == KC - 1)), reads=[bW, b_HN], writes=[bpz])
                            zevac(cb * 4 + cl, pz, bpz)
                for cb in range(6):
                    W, bW = load_w(s_winb[cb], wb[("winb", cb)])
                    Wv = W[:, :].rearrange("p (kc c) -> p kc c", kc=KC)
                    for sub in range(NT // 128):
                        pz, bpz = next_ps()
                        for kc in range(KC):
                            P.op("pe", lambda h, kc=kc, sub=sub, pz=pz, Wv=Wv: h.matmul(pz[:, :], HN[:, kc, sub * 128:(sub + 1) * 128], Wv[:, kc, :],
                                 start=(kc == 0), stop=(kc == KC - 1)), reads=[bW, b_HN], writes=[bpz])
                        sg, bsg = next_stg()
                        if cb >= 4:
                            P.op("act", lambda h, sg=sg, pz=pz: h.activation(out=sg[:, :], in_=pz[:, :], func=AF.Sigmoid), reads=[bpz], writes=[bsg])
                        else:
                            P.op("act", lambda h, sg=sg, pz=pz: h.copy(out=sg[:, :], in_=pz[:, :]), reads=[bpz], writes=[bsg])
                        r0 = HALO + t0 + sub * 128
                        P.dma("act", s_ztm[r0:r0 + 128, cb * 512:(cb + 1) * 512], sg[:, :], reads=[bsg], writes=[Buf('zst')])
                for sub in range(NT // 128):
                    pz, bpz = next_ps()
                    for kc in range(KC):
                        P.op("pe", lambda h, kc=kc, sub=sub, pz=pz: h.matmul(pz[:, 0:NG], HN[:, kc, sub * 128:(sub + 1) * 128], wing_sb[:, kc * NG:(kc + 1) * NG],
                             start=(kc == 0), stop=(kc == KC - 1)), reads=[b_c, b_HN], writes=[bpz])
                    ch = (t0 + sub * 128) // 128
                    P.op("dve", lambda h, pz=pz, ch=ch: h.tensor_tensor(out=Gtm[:, ch, :], in0=pz[:, 0:NG], in1=bgr_sb[:, :], op=ALU.add),
                         reads=[bpz, b_c], writes=[b_G])
                if it == 0:
                    emit_pads()
                    deferred_casts()
        else:
            b_y = Buf("yout")
            P.dma("sp", YTs[:, :, :], syT_v[:, :, 0:NT], reads=[b_yT], writes=[b_YT])
            P.dma("sp", PTs[:, :, :], spT_v[:, :, 0:NT], reads=[b_pT], writes=[b_PT])
            for it in range(NTILE):
                t0 = it * NT
                P.dma("sp", X[:, :, :], x1_v[:, :, t0:t0 + NT], reads=[b_x1[it]], writes=[b_X])
                for mb in range(4):
                    W, bW = load_w(s_wout[mb], wb[("wout", mb)])
                    Wv = W[:, :].rearrange("p (ml kc c) -> p ml kc c", ml=4, kc=KC)
                    for ml in range(4):
                        m = mb * 4 + ml
                        po, bpo = next_ps()
                        for kc in range(KC):
                            P.op("pe", lambda h, kc=kc, ml=ml, po=po, Wv=Wv: h.matmul(po[:, :], Wv[:, ml, kc, :], YTs[:, kc, :],
                                 start=(kc == 0), stop=(kc == KC - 1)), reads=[bW, b_YT], writes=[bpo])
                        flush_sq()
                        P.op("dve", lambda h, m=m, po=po: h.tensor_tensor(out=X[:, m, :], in0=po[:, :], in1=X[:, m, :], op=ALU.add),
                             reads=[bpo, b_X], writes=[b_X])
                        fused_sq(m)
                if it + 1 < NTILE:
                    P.dma("sp", YTs[:, :, :], syT_v[:, :, t0 + NT:t0 + 2 * NT], reads=[b_yT], writes=[b_YT])
                rmsnorm_to_HN(2, fused=True)
                ffn(s_w2in, s_w2out, "w2in", "w2out")
                rmsnorm_to_HN(3, fused=True)
                for mb in range(4):
                    W, bW = load_w(s_wgate[mb], wb[("wgate", mb)])
                    Wv = W[:, :].rearrange("p (ml kc c) -> p ml kc c", ml=4, kc=KC)
                    W2, bW2 = load_w(s_wproj[mb], wb[("wproj", mb)], nelem=1024)
                    W2v = W2[:, 0:1024].rearrange("p (ml kc c) -> p ml kc c", ml=4, kc=2)
                    for ml in range(4):
                        m = mb * 4 + ml
                        pg, bpg = next_ps()
                        pp, bpp = next_ps()
                        for kc in range(KC):
                            P.op("pe", lambda h, kc=kc, ml=ml, pg=pg, Wv=Wv: h.matmul(pg[:, :], Wv[:, ml, kc, :], HN[:, kc, :],
                                 start=(kc == 0), stop=(kc == KC - 1)), reads=[bW, b_HN], writes=[bpg])
                        for kc in range(2):
                            P.op("pe", lambda h, kc=kc, ml=ml, pp=pp, W2v=W2v: h.matmul(pp[:, :], W2v[:, ml, kc, :], PTs[:, kc, :],
                                 start=(kc == 0), stop=(kc == 1)), reads=[bW2, b_PT], writes=[bpp])
                        tt, bt = next_tmp()
                        P.op("act", lambda h, tt=tt, pg=pg: h.activation(out=tt[:, :], in_=pg[:, :], func=AF.Sigmoid), reads=[bpg], writes=[bt])
                        t2, bt2 = next_tmp()
                        P.op("dve", lambda h, tt=tt, t2=t2, pp=pp: h.tensor_tensor(out=t2[:, :], in0=tt[:, :], in1=pp[:, :], op=ALU.mult),
                             reads=[bt, bpp], writes=[bt2])
                        P.op("dve", lambda h, m=m, t2=t2: h.tensor_tensor(out=X[:, m, :], in0=t2[:, :], in1=X[:, m, :], op=ALU.add),
                             reads=[bt2, b_X], writes=[b_X])
                if it + 1 < NTILE:
                    P.dma("sp", PTs[:, :, :], spT_v[:, :, t0 + NT:t0 + 2 * NT], reads=[b_pT], writes=[b_PT])
                rms_stats()
                for kc in range(KC):
                    gcol = nrm_sb[:, 4 * KC + kc: 4 * KC + kc + 1]
                    P.op("dve", lambda h, kc=kc, gcol=gcol: h.scalar_tensor_tensor(
                        out=X[:, kc, :], in0=X[:, kc, :], scalar=gcol, in1=RSTD[:, :], op0=ALU.mult, op1=ALU.mult),
                        reads=[b_X, b_RSTD, b_c], writes=[b_X])
                P.dma("sp", yT_v[:, :, t0:t0 + NT], X[:, :, :], reads=[b_X], writes=[b_y])
        P.barrier()
        dense.close()

    def attention_phase():
        sc = ExitStack()
        def asb(name, shape, dt):
            return sc.enter_context(nc.sbuf_tensor("att_" + name, list(shape), dt))
        NTL = {1: 17, 4: 5, 16: 2}
        QT = [asb("QT%d" % k, [128, SEG], BF16) for k in range(2)]
        KT = [asb("KT%d" % k, [128, HALO + SEG + HALO], BF16) for k in range(2)]
        VT = [{d: asb("VT%d_%d" % (d, k), [128, d * NTL[d], 128], BF16) for d in DIL} for k in range(2)]
        b_LD = [dict() for k in range(2)]
        RB = [asb("RB%d" % k, [128, 6, 128], F32) for k in range(2)]; b_RB = [Buf("RB%d" % k) for k in range(2)]
        EXPB = asb("EXPB", [128, 6, 128], F32); DIFF = asb("DIFF", [128, 6, 128], F32); b_EW = Buf("EW")
        EBI = [asb("EBI%d" % k, [128, 6, 128], BF16) for k in range(2)]
        EBB = [asb("EBB%d" % k, [128, 6, 128], BF16) for k in range(2)]
        EBX = [asb("EBX%d" % k, [128, 6, 128], BF16) for k in range(2)]
        b_EB = [Buf("EB%d" % k) for k in range(2)]
        NET = 6
        ET = [asb("ET%d" % i, [128, 2, 128], F32) for i in range(NET)]; b_ET = [Buf("ET%d" % i) for i in range(NET)]
        PTt = [asb("PTt%d" % i, [128, 2, 128], BF16) for i in range(NET)]; b_PTt = [Buf("PTt%d" % i) for i in range(NET)]
        ACCN = asb("ACCN", [128, SEG], F32); ACCD = asb("ACCD", [128, SEG], F32)
        b_ACC = [Buf("ACC0"), Buf("ACC1")]
        ATT = asb("ATT", [128, 8, SEG], F32); b_ATT = Buf("ATT")
        SQ = asb("SQ", [128, NT], BF16); b_SQ = Buf("SQ")
        R1 = asb("R1", [128, NT], F32); R2 = asb("R2", [128, NT], F32); b_R = Buf("R")
        YS = [asb("YS%d" % i, [128, NT], BF16) for i in range(2)]; b_YS = [Buf("YS%d" % i) for i in range(2)]
        rr = {"et": 0, "ys": 0, "ld": 0, "eb": 0}
        LAG = 4

        for seg in range(NSEG):
            sb0 = seg * SEG
            for pr in range(ATT_PAIRS):
                k = rr["ld"] % 2; rr["ld"] += 1
                QTk, KTk, VTk, bL = QT[k], KT[k], VT[k], b_LD[k]
                if not bL:
                    bL["q"] = Buf("ldq%d" % k); bL["k"] = Buf("ldk%d" % k)
                    for d_ in DIL:
                        for r_ in range(d_):
                            bL[(d_, r_)] = Buf("ldv%d_%d_%d" % (k, d_, r_))
                P.dma("sp", QTk[:, :], zfm_v[:, pr, HALO + sb0:HALO + sb0 + SEG], reads=[b_z], writes=[bL["q"]])
                P.dma("sp", KTk[:, :], zfm_v[:, 8 + pr, sb0:sb0 + SEG + 2 * HALO], reads=[b_z, b_pad], writes=[bL["k"]])
                for d in DIL:
                    for r in range(d):
                        base_row = HALO + sb0 - 64 * d + r
                        src = s_ztm[base_row:base_row + d * (128 * NTL[d] - 1) + 1:d, pr * 128:(pr + 1) * 128]
                        P.dma("sp", VTk[d][:, r * NTL[d]:(r + 1) * NTL[d], :], src.rearrange("(n p) c -> p n c", p=128),
                              reads=[b_z, b_pad], writes=[bL[(d, r)]])
                hctx = []
                for hh in range(2):
                    head = pr * 2 + hh
                    e = rr["eb"] % 2; rr["eb"] += 1
                    RBe, EBIe, EBBe, EBXe, bE = RB[e], EBI[e], EBB[e], EBX[e], b_EB[e]
                    P.dma("sp", RBe[:, :, :], relb[head], writes=[b_RB[e]])
                    P.op("act", lambda h, RBe=RBe: h.activation(out=EXPB[:, :, :], in_=RBe[:, :, :], func=AF.Exp), reads=[b_RB[e]], writes=[b_EW])
                    P.op("dve", lambda h, EBIe=EBIe: h.tensor_tensor(out=EBIe[:, :, :], in0=EXPB[:, :, :], in1=cm_sb[:, 0:6, :], op=ALU.mult), reads=[b_EW, b_c], writes=[bE])
                    P.op("dve", lambda h, EBBe=EBBe: h.tensor_tensor(out=EBBe[:, :, :], in0=EXPB[:, :, :], in1=cm_sb[:, 6:12, :], op=ALU.mult), reads=[b_EW, b_c], writes=[bE])
                    P.op("dve", lambda h, EBBe=EBBe, EBIe=EBIe: h.tensor_tensor(out=DIFF[:, :, :], in0=EBBe[:, :, :], in1=EBIe[:, :, :], op=ALU.subtract), reads=[bE], writes=[b_EW])
                    P.op("dve", lambda h, EBXe=EBXe, EBIe=EBIe: h.scalar_tensor_tensor(out=EBXe[:, :, :], in0=DIFF[:, :, :], scalar=sep_sb[:, 0:1], in1=EBIe[:, :, :],
                         op0=ALU.mult, op1=ALU.add), reads=[b_EW, bE, b_c], writes=[bE])
                    hctx.append(dict(hs=slice(hh * 64, hh * 64 + 64), bA=b_ACC[hh], EBI=EBIe, EBB=EBBe, EBX=EBXe, bE=bE, state={}))
                per_head = []
                for hh in range(2):
                    lst = []
                    for bi, d in enumerate(DIL):
                        if d not in ATT_DILS:
                            continue
                        nqt = (SEG // d) // 128
                        tl = [(r, n) for r in range(d) for n in range(nqt)]
                        for b0 in range(0, len(tl), 2):
                            for qi in range(2):
                                r, n = tl[b0 + qi]
                                lst.append(dict(hh=hh, bi=bi, d=d, r=r, n=n, nqt=nqt, qi=qi, first=tl[b0]))
                    per_head.append(lst)
                items = []
                for i2_ in range(0, len(per_head[0]), 2):
                    items += per_head[0][i2_:i2_ + 2] + per_head[1][i2_:i2_ + 2]
                def stage1(it):
                    d, r, n, bi, nqt = it["d"], it["r"], it["n"], it["bi"], it["nqt"]
                    hc = hctx[it["hh"]]; hs = hc["hs"]; bE = hc["bE"]
                    qcols = QTk[hs, d * 128 * n + r: d * 128 * n + r + d * 127 + 1: d]
                    pss, bps = next_ps()
                    for tau in range(2):
                        kstart = HALO + d * (-64 + 128 * (n + tau)) + r
                        kcols = KTk[hs, kstart:kstart + d * 127 + 1:d]
                        P.op("pe", lambda h, pss=pss, tau=tau, kcols=kcols, qcols=qcols: h.matmul(
                            pss[:, tau * 128:(tau + 1) * 128], kcols, qcols, start=True, stop=True),
                            reads=[bL["q"], bL["k"]], writes=[bps])
                    ei = rr["et"] % NET; rr["et"] += 1
                    E = ET[ei]; PT_ = PTt[ei]
                    P.op("act", lambda h, E=E, pss=pss: h.activation(out=E[:, :, :], in_=pss[:, 0:256].rearrange("p (a b) -> p a b", a=2),
                         func=AF.Exp, scale=0.125), reads=[bps], writes=[b_ET[ei]])
                    ebs = []
                    for tau in range(2):
                        if tau == 0 and n == 0:
                            ebs.append(hc["EBB"] if seg == 0 else hc["EBX"])
                        elif tau == 1 and n == nqt - 1:
                            ebs.append(hc["EBB"] if seg == NSEG - 1 else hc["EBX"])
                        else:
                            ebs.append(hc["EBI"])
                    if ebs[0] is ebs[1]:
                        eb = ebs[0]
                        P.op("dve", lambda h, PT_=PT_, E=E, eb=eb, bi=bi: h.tensor_tensor(
                            out=PT_[:, :, :], in0=E[:, :, :], in1=eb[:, bi * 2:bi * 2 + 2, :], op=ALU.mult),
                            reads=[b_ET[ei], bE], writes=[b_PTt[ei]])
                    else:
                        for tau in range(2):
                            eb = ebs[tau]
                            P.op("dve", lambda h, PT_=PT_, E=E, eb=eb, tau=tau, bi=bi: h.tensor_tensor(
                                out=PT_[:, tau, :], in0=E[:, tau, :], in1=eb[:, bi * 2 + tau, :], op=ALU.mult),
                                reads=[b_ET[ei], bE], writes=[b_PTt[ei]])
                    it["PT"] = PT_; it["bPT"] = b_PTt[ei]
                def stage2(it):
                    d, r, n, bi, qi = it["d"], it["r"], it["n"], it["bi"], it["qi"]
                    hc = hctx[it["hh"]]; hs = hc["hs"]; bA = hc["bA"]; state = hc["state"]
                    if qi == 0:
                        state["pnd"], state["bpnd"] = next_ps()
                    pnd, bpnd = state["pnd"], state["bpnd"]
                    PT_ = it["PT"]
                    for tau in range(2):
                        vt = VTk[d][:, r * NTL[d] + n + tau, :]
                        P.op("pe", lambda h, pnd=pnd, qi=qi, vt=vt, PT_=PT_, tau=tau: h.matmul(
                            pnd[:, qi * 128:(qi + 1) * 128], vt, PT_[:, tau, :], start=(tau == 0), stop=(tau == 1)),
                            reads=[bL[(d, r)], it["bPT"]], writes=[bpnd])
                    for tau in range(2):
                        P.op("pe", lambda h, pnd=pnd, qi=qi, PT_=PT_, tau=tau: h.matmul(
                            pnd[:, 256 + qi * 128:256 + (qi + 1) * 128], ones_bf[:, :], PT_[:, tau, :], start=(tau == 0), stop=(tau == 1)),
                            reads=[b_c, it["bPT"]], writes=[bpnd])
                    if qi == 1:
                        r0, n0 = it["first"]
                        if d == 1:
                            oN = ACCN[hs, 128 * n0:128 * n0 + 256]; oD = ACCD[hs, 128 * n0:128 * n0 + 256]
                            iN = pnd[hs, 0:256]; iD = pnd[hs, 256:512]
                        elif d == 4:
                            t_0 = r0 + 4 * 128 * n0
                            oN = ACCN[hs, t_0:t_0 + 4 * 255 + 1:4]; oD = ACCD[hs, t_0:t_0 + 4 * 255 + 1:4]
                            iN = pnd[hs, 0:256]; iD = pnd[hs, 256:512]
                        else:
                            oN = ACCN[hs, :].rearrange("p (j r) -> p r j", r=16)[:, r0:r0 + 2, :]
                            oD = ACCD[hs, :].rearrange("p (j r) -> p r j", r=16)[:, r0:r0 + 2, :]
                            iN = pnd[hs, 0:256].rearrange("p (a b) -> p a b", a=2); iD = pnd[hs, 256:512].rearrange("p (a b) -> p a b", a=2)
                        if bi == 0:
                            P.op("act", lambda h, oN=oN, iN=iN: h.copy(out=oN, in_=iN), reads=[bpnd], writes=[bA])
                            P.op("act", lambda h, oD=oD, iD=iD: h.copy(out=oD, in_=iD), reads=[bpnd], writes=[bA])
                        else:
                            P.op("dve", lambda h, oN=oN, iN=iN: h.tensor_tensor(out=oN, in0=iN, in1=oN, op=ALU.add), reads=[bpnd, bA], writes=[bA])
                            P.op("dve", lambda h, oD=oD, iD=iD: h.tensor_tensor(out=oD, in0=iD, in1=oD, op=ALU.add), reads=[bpnd, bA], writes=[bA])
                for idx in range(len(items) + LAG):
                    if idx < len(items):
                        stage1(items[idx])
                    if idx - LAG >= 0:
                        stage2(items[idx - LAG])
                for hh in range(2):
                    hs = hctx[hh]["hs"]; bA = hctx[hh]["bA"]
                    P.op("dve", lambda h, hs=hs: h.reciprocal(out=ACCD[hs, :], in_=ACCD[hs, :]), reads=[bA], writes=[bA])
                    P.op("dve", lambda h, hs=hs, pr=pr: h.tensor_tensor(out=ATT[hs, pr, :], in0=ACCN[hs, :], in1=ACCD[hs, :], op=ALU.mult),
                         reads=[bA], writes=[b_ATT])
            for blk in range(SEG // NT):
                cs = slice(blk * NT, (blk + 1) * NT)
                pt, bp = next_ps()
                for pr in range(8):
                    P.op("act", lambda h, pr=pr, cs=cs: h.activation(out=SQ[:, :], in_=ATT[:, pr, cs], func=AF.Square), reads=[b_ATT], writes=[b_SQ])
                    P.op("pe", lambda h, pr=pr, pt=pt: h.matmul(pt[:, :], ones_bf[:, :], SQ[:, :], start=(pr == 0), stop=(pr == 7)),
                         reads=[b_SQ, b_c], writes=[bp])
                P.op("act", lambda h, pt=pt: h.activation(out=R1[:, :], in_=pt[:, :], func=AF.Sqrt, bias=eps_sb[:, 0:1], scale=1.0 / AW),
                     reads=[bp, b_c], writes=[b_R])
                P.op("dve", lambda h: h.reciprocal(out=R2[:, :], in_=R1[:, :]), reads=[b_R], writes=[b_R])
                for pr in range(8):
                    yi = rr["ys"] % 2; rr["ys"] += 1
                    P.op("dve", lambda h, pr=pr, cs=cs, yi=yi: h.scalar_tensor_tensor(out=YS[yi][:, :], in0=ATT[:, pr, cs], scalar=anrm_sb[:, pr:pr + 1],
                         in1=R2[:, :], op0=ALU.mult, op1=ALU.mult), reads=[b_ATT, b_R, b_c], writes=[b_YS[yi]])
                    P.dma("sp", syT_v[:, pr, sb0 + blk * NT:sb0 + (blk + 1) * NT], YS[yi][:, :], reads=[b_YS[yi]], writes=[Buf('yst')])
        P.barrier()
        sc.close()

    def scan_setup():
        sc = ExitStack()
        def ssb(name, shape, dt):
            return sc.enter_context(nc.sbuf_tensor("scn_" + name, list(shape), dt))
        LI = ssb("LI", [128, NCH, 8], F32); FP = ssb("FP", [128, NCH, 8], F32); LF = ssb("LF", [128, NCH, 8], F32)
        PL = ssb("PL", [128, NCH, 8], F32); TOT = ssb("TOT", [128, NCH, 8], F32)
        OFFI = ssb("OFFI", [128, 8, NCH], F32); ONESC = ssb("ONESC", [128, NCH], F32)
        UT = ssb("UT", [128, NB, 128], F32); UMC = ssb("UMC", [128, NB], F32)
        PRM = ssb("PRM", [128, NQ], F32); PRS = ssb("PRS", [128, NQ], F32)
        UMR = ssb("UMR", [128, 8, NCH], F32); RR = ssb("RR", [128, NCH], F32)
        MC = ssb("MC", [128, 8, NCH], F32); MSH = ssb("MSH", [128, 8, NCH], F32); SEGM = ssb("SEGM", [128, NCH], F32)
        b = Buf("scanwork")
        gv = Gtm[:, :, :].rearrange("p c (g h) -> p c g h", g=4)
        P.dma("sp", SEGM[:, :], segm[:, :], writes=[b])
        P.op("dve", lambda h: h.memset(ONESC[:, :], 1.0), writes=[b])
        P.op("dve", lambda h: h.tensor_copy(out=LI[:, :, 0:4], in_=gv[:, :, 0, :]), reads=[b_G], writes=[b])
        P.op("dve", lambda h: h.tensor_copy(out=LI[:, :, 4:8], in_=gv[:, :, 2, :]), reads=[b_G], writes=[b])
        P.op("dve", lambda h: h.tensor_copy(out=FP[:, :, 0:4], in_=gv[:, :, 1, :]), reads=[b_G], writes=[b])
        P.op("dve", lambda h: h.tensor_copy(out=FP[:, :, 4:8], in_=gv[:, :, 3, :]), reads=[b_G], writes=[b])
        P.op("act", lambda h: h.activation(out=LF[:, :, :], in_=FP[:, :, :], func=AF.Exp, scale=-1.0), reads=[b], writes=[b])
        P.op("dve", lambda h: h.tensor_scalar(out=LF[:, :, :], in0=LF[:, :, :], scalar1=1.0, scalar2=None, op0=ALU.add), reads=[b], writes=[b])
        P.op("act", lambda h: h.activation(out=LF[:, :, :], in_=LF[:, :, :], func=AF.Ln), reads=[b], writes=[b])
        P.op("dve", lambda h: h.tensor_scalar(out=LF[:, :, :], in0=LF[:, :, :], scalar1=-1.0, scalar2=None, op0=ALU.mult), reads=[b], writes=[b])
        pt, bp = next_ps()
        P.op("pe", lambda h: h.matmul(pt[:, 0:NQ], cf_sb[:, 0, :], LF[:, :, :].rearrange("p c h -> p (c h)"), start=True, stop=True),
             reads=[b, b_c], writes=[bp])
        P.op("dve", lambda h: h.tensor_copy(out=PL[:, :, :].rearrange("p c h -> p (c h)"), in_=pt[:, 0:NQ]), reads=[bp], writes=[b])
        pt2, bp2 = next_ps()
        P.op("pe", lambda h: h.matmul(pt2[:, 0:NQ], cf_sb[:, 1, :], PL[:, :, :].rearrange("p c h -> p (c h)"), start=True, stop=True),
             reads=[b, b_c], writes=[bp2])
        P.op("dve", lambda h: h.tensor_copy(out=TOT[:, :, :].rearrange("p c h -> p (c h)"), in_=pt2[:, 0:NQ]), reads=[bp2], writes=[b])
        for hd in range(8):
            P.op("dve", lambda h, hd=hd: h.tensor_tensor_scan(out=OFFI[:, hd, :], data0=ONESC[:, :], data1=TOT[:, :, hd], initial=0.0,
                 op0=ALU.mult, op1=ALU.add), reads=[b], writes=[b])
        P.op("dve", lambda h: h.tensor_tensor(out=PL[:, :, :], in0=PL[:, :, :], in1=OFFI[:, :, :].rearrange("p h c -> p c h"), op=ALU.add), reads=[b], writes=[b])
        P.op("dve", lambda h: h.tensor_tensor(out=PL[:, :, :], in0=PL[:, :, :], in1=TOT[:, :, :], op=ALU.subtract), reads=[b], writes=[b])
        P.op("dve", lambda h: h.tensor_copy(out=Fg[:, :, 0:4], in_=PL[:, :, 0:4]), reads=[b], writes=[b_scan])
        P.op("dve", lambda h: h.tensor_tensor(out=Fg[:, :, 4:8], in0=LF[:, :, 4:8], in1=PL[:, :, 4:8], op=ALU.subtract), reads=[b], writes=[b_scan])
        P.op("dve", lambda h: h.tensor_tensor(out=Ug[:, :, :], in0=LI[:, :, :], in1=Fg[:, :, :], op=ALU.subtract), reads=[b, b_scan], writes=[b_scan])
        uflat = Ug[:, :, :].rearrange("p c h -> p (c h)")
        for nb in range(NB):
            ptt, bpt = next_ps()
            P.op("pe", lambda h, nb=nb, ptt=ptt: h.transpose(out=ptt[:, 0:128], in_=uflat[:, nb * 128:(nb + 1) * 128], identity=cf_sb[:, 2, :]),
                 reads=[b_scan, b_c], writes=[bpt])
            P.op("dve", lambda h, nb=nb, ptt=ptt: h.tensor_reduce(out=UMC[:, nb:nb + 1], in_=ptt[:, 0:128], axis=AX.X, op=ALU.max), reads=[bpt], writes=[b])
        pu, bpu = next_ps()
        for nb in range(NB):
            P.dma("sp", PRM[:, :], perm[nb], writes=[b])
            P.op("dve", lambda h, nb=nb: h.tensor_scalar(out=PRS[:, :], in0=PRM[:, :], scalar1=UMC[:, nb:nb + 1], scalar2=None, op0=ALU.mult), reads=[b], writes=[b])
            P.op("pe", lambda h, nb=nb, pu=pu: h.matmul(pu[:, 0:NQ], ones_f[:, :], PRS[:, :], start=(nb == 0), stop=(nb == NB - 1)), reads=[b, b_c], writes=[bpu])
        P.op("dve", lambda h: h.tensor_copy(out=UMR[:, :, :].rearrange("p h c -> p (h c)"), in_=pu[:, 0:NQ]), reads=[bpu], writes=[b])
        P.op("dve", lambda h: h.tensor_scalar(out=RR[:, :], in0=SEGM[:, :], scalar1=sepneg[:, 0:1], scalar2=None, op0=ALU.mult), reads=[b, b_c], writes=[b])
        for hd in range(8):
            P.op("dve", lambda h, hd=hd: h.tensor_tensor_scan(out=MC[:, hd, :], data0=RR[:, :], data1=UMR[:, hd, :], initial=NEGBIG,
                 op0=ALU.add, op1=ALU.max), reads=[b], writes=[b])
        P.op("dve", lambda h: h.tensor_scalar(out=NEGMC[:, :, :], in0=MC[:, :, :], scalar1=-1.0, scalar2=None, op0=ALU.mult), reads=[b], writes=[b_scan])
        P.op("dve", lambda h: h.memset(MSH[:, :, :], NEGBIG), writes=[b])
        P.op("dve", lambda h: h.tensor_copy(out=MSH[:, :, 1:NCH], in_=MC[:, :, 0:NCH - 1]), reads=[b], writes=[b])
        P.op("dve", lambda h: h.tensor_tensor(out=MSH[:, :, :], in0=MSH[:, :, :], in1=RR[:, :].unsqueeze(1).to_broadcast([128, 8, NCH]), op=ALU.add), reads=[b], writes=[b])
        P.op("dve", lambda h: h.tensor_tensor(out=MSH[:, :, :], in0=MSH[:, :, :], in1=MC[:, :, :], op=ALU.subtract), reads=[b], writes=[b])
        P.op("dve", lambda h: h.tensor_scalar(out=MSH[:, :, :], in0=MSH[:, :, :], scalar1=-80.0, scalar2=None, op0=ALU.max), reads=[b], writes=[b])
        P.op("act", lambda h: h.activation(out=DEC[:, :, :], in_=MSH[:, :, :], func=AF.Exp), reads=[b], writes=[b_scan])
        P.barrier()
        sc.close()

    def mlstm_phase():
        sc = ExitStack()
        def msb(name, shape, dt):
            return sc.enter_context(nc.sbuf_tensor("ml_" + name, list(shape), dt))
        QT = msb("QT", [128, 2, T], BF16); KT = msb("KT", [128, 2, T], BF16); b_QK = Buf("QK")
        KTM = msb("KTM", [128, NCH, 256], BF16); b_KTM = Buf("KTM")
        XP = msb("XP", [128, SEG + 2], BF16); b_XP = Buf("XP")
        YC = msb("YC", [128, SEG], F32); b_YC = Buf("YC")
        TC = msb("TC", [128, 1], F32); b_TC = Buf("TC")
        NR = 6
        VB = [msb("VB%d" % i, [128, 256], BF16) for i in range(NR)]; b_VB = [Buf("VB%d" % i) for i in range(NR)]
        OB = [msb("OB%d" % i, [128, 256], BF16) for i in range(NR)]; b_OB = [Buf("OB%d" % i) for i in range(NR)]
        HFB = [msb("HFB%d" % i, [128, 256], F32) for i in range(NR)]; b_HFB = [Buf("HFB%d" % i) for i in range(NR)]
        NA = 4
        VP = [msb("VP%d" % i, [128, 257], BF16) for i in range(NA)]; b_VP = [Buf("VP%d" % i) for i in range(NA)]
        WT = [msb("WT%d" % i, [128, 128], BF16) for i in range(NA)]; b_WT = [Buf("WT%d" % i) for i in range(NA)]
        CST = [msb("CST%d" % i, [128, 2, 257], F32) for i in range(2)]; b_CST = [Buf("CST%d" % i) for i in range(2)]
        CPB = [msb("CPB%d" % i, [128, 2, 257], BF16) for i in range(NA)]; b_CPB = [Buf("CPB%d" % i) for i in range(NA)]
        SC1 = [msb("SC1_%d" % i, [128, 8], F32) for i in range(NA)]; b_SC1 = [Buf("SC1_%d" % i) for i in range(NA)]
        HH = [msb("HH%d" % i, [128, 256], F32) for i in range(NA)]; b_HH = [Buf("HH%d" % i) for i in range(NA)]
        SG = [msb("SG%d" % i, [128, 256], F32) for i in range(NA)]; b_SG = [Buf("SG%d" % i) for i in range(NA)]
        BST = [msb("BST%d" % i, [128, 8], F32) for i in range(NA)]; b_BST = [Buf("BST%d" % i) for i in range(NA)]
        YB = [msb("YB%d" % i, [128, 256], BF16) for i in range(NA)]; b_YB = [Buf("YB%d" % i) for i in range(NA)]
        YO = [msb("YO%d" % i, [128, 2, 128], BF16) for i in range(NA)]; b_YO = [Buf("YO%d" % i) for i in range(NA)]
        MN = msb("MN", [128, MW], F32); b_MN = Buf("MN")
        P.dma("sp", MN[:, :], mnrm[:, :], writes=[b_MN])
        b_hf = Buf("s_hf")
        rr = [0]

        for head in range(4):
            for qk in range(2):
                dst = QT if qk == 0 else KT
                for ft in range(2):
                    cidx = qk * 8 + head * 2 + ft
                    zc = 16 + qk * 8 + head * 2 + ft
                    for seg in range(NSEG):
                        sb0 = seg * SEG
                        P.dma("sp", XP[:, :], zfm_v[:, zc, HALO + sb0 - 1:HALO + sb0 + SEG + 1], reads=[b_z, b_pad], writes=[b_XP])
                        P.op("dve", lambda h, cidx=cidx: h.tensor_scalar(out=YC[:, :], in0=XP[:, 1:SEG + 1], scalar1=cvw_sb[:, cidx, 1:2], scalar2=None, op0=ALU.mult),
                             reads=[b_XP, b_c], writes=[b_YC])
                        P.op("dve", lambda h, cidx=cidx: h.scalar_tensor_tensor(out=YC[:, :], in0=XP[:, 0:SEG], scalar=cvw_sb[:, cidx, 0:1], in1=YC[:, :],
                             op0=ALU.mult, op1=ALU.add), reads=[b_XP, b_c, b_YC], writes=[b_YC])
                        P.op("dve", lambda h, cidx=cidx: h.scalar_tensor_tensor(out=YC[:, :], in0=XP[:, 2:SEG + 2], scalar=cvw_sb[:, cidx, 2:3], in1=YC[:, :],
                             op0=ALU.mult, op1=ALU.add), reads=[b_XP, b_c, b_YC], writes=[b_YC])
                        if seg > 0:
                            P.op("dve", lambda h, cidx=cidx: h.tensor_scalar(out=TC[:, :], in0=XP[:, 0:1], scalar1=w0s[:, cidx:cidx + 1], scalar2=None, op0=ALU.mult),
                                 reads=[b_XP, b_c], writes=[b_TC])
                            P.op("dve", lambda h: h.tensor_tensor(out=YC[:, 0:1], in0=YC[:, 0:1], in1=TC[:, :], op=ALU.subtract), reads=[b_TC, b_YC], writes=[b_YC])
                        if seg < NSEG - 1:
                            P.op("dve", lambda h, cidx=cidx: h.tensor_scalar(out=TC[:, :], in0=XP[:, SEG + 1:SEG + 2], scalar1=w2s[:, cidx:cidx + 1], scalar2=None, op0=ALU.mult),
                                 reads=[b_XP, b_c], writes=[b_TC])
                            P.op("dve", lambda h: h.tensor_tensor(out=YC[:, SEG - 1:SEG], in0=YC[:, SEG - 1:SEG], in1=TC[:, :], op=ALU.subtract), reads=[b_TC, b_YC], writes=[b_YC])
                        if qk == 0:
                            P.op("act", lambda h, cidx=cidx: h.activation(out=YC[:, :], in_=YC[:, :], func=AF.Silu, bias=cvb_sb[:, cidx:cidx + 1]), reads=[b_YC, b_c], writes=[b_YC])
                            P.op("dve", lambda h, ft=ft, sb0=sb0: h.tensor_scalar(out=QT[:, ft, sb0:sb0 + SEG], in0=YC[:, :], scalar1=1.0 / 16.0, scalar2=None, op0=ALU.mult),
                                 reads=[b_YC], writes=[b_QK])
                        else:
                            P.op("act", lambda h, cidx=cidx, ft=ft, sb0=sb0: h.activation(out=KT[:, ft, sb0:sb0 + SEG], in_=YC[:, :], func=AF.Silu, bias=cvb_sb[:, cidx:cidx + 1]),
                                 reads=[b_YC, b_c], writes=[b_QK])
            for c in range(NCH):
                po, bpo = next_pst()
                for ft in range(2):
                    P.op("pe", lambda h, c=c, ft=ft, po=po: h.transpose(out=po[:, ft, :], in_=KT[:, ft, c * 128:(c + 1) * 128], identity=ident_bf[:, :]),
                         reads=[b_QK, b_c], writes=[bpo])
                P.op("act", lambda h, c=c, po=po: h.copy(out=KTM[:, c, :].rearrange("p (a b) -> p a b", a=2), in_=po[:, 0:2, :]), reads=[bpo], writes=[b_KTM])
            for dr in range(2):
                P.op("dve", lambda h, dr=dr: h.memset(CST[dr][:, :, :], 0.0), writes=[b_CST[dr]])
            def stageA(dr, cc):
                hd = dr * 4 + head
                c = cc if dr == 0 else NCH - 1 - cc
                tk = slice(c * 128, (c + 1) * 128)
                second = cc >= NCH // 2
                i3 = rr[0] % NR; i2 = rr[0] % NA; rr[0] += 1
                r0 = HALO + c * 128
                P.dma("sp", VB[i3][:, :], s_ztm[r0:r0 + 128, 1024 + head * 256:1024 + (head + 1) * 256], reads=[b_z], writes=[b_VB[i3]])
                if second:
                    P.dma("sp", OB[i3][:, :], s_ztm[r0:r0 + 128, 2048 + head * 256:2048 + (head + 1) * 256], reads=[b_z], writes=[b_OB[i3]])
                s1 = SC1[i2]
                P.op("act", lambda h: h.activation(out=s1[:, 0:1], in_=Ug[:, c, hd:hd + 1], func=AF.Exp, bias=NEGMC[:, hd, cc:cc + 1]),
                     reads=[b_scan], writes=[b_SC1[i2]])
                P.op("act", lambda h: h.activation(out=s1[:, 1:2], in_=Fg[:, c, hd:hd + 1], func=AF.Exp, bias=NEGMC[:, hd, cc:cc + 1], scale=-1.0),
                     reads=[b_scan], writes=[b_SC1[i2]])
                vp = VP[i2]
                P.op("dve", lambda h: h.tensor_scalar(out=vp[:, 0:256], in0=VB[i3][:, :], scalar1=s1[:, 0:1], scalar2=None, op0=ALU.mult),
                     reads=[b_VB[i3], b_SC1[i2]], writes=[b_VP[i2]])
                P.op("dve", lambda h: h.tensor_copy(out=vp[:, 256:257], in_=s1[:, 0:1]), reads=[b_SC1[i2]], writes=[b_VP[i2]])
                pS, bpS = next_ps()
                for ft in range(2):
                    P.op("pe", lambda h, ft=ft: h.matmul(pS[:, 0:128], KT[:, ft, tk], QT[:, ft, tk], start=(ft == 0), stop=(ft == 1)),
                         reads=[b_QK], writes=[bpS])
                wt = WT[i2]
                P.op("dve", lambda h: h.tensor_tensor(out=wt[:, :], in0=pS[:, 0:128], in1=cmc_bf[:, dr, :], op=ALU.mult),
                     reads=[bpS, b_c], writes=[b_WT[i2]])
                return dict(dr=dr, cc=cc, c=c, hd=hd, tk=tk, second=second, i3=i3, i2=i2)
            def stageB(ctx):
                dr, cc, c, hd, tk, second, i3, i2 = (ctx[k] for k in ("dr", "cc", "c", "hd", "tk", "second", "i3", "i2"))
                s1 = SC1[i2]; vp = VP[i2]; wt = WT[i2]
                cst = CST[dr]; bcst = b_CST[dr]
                if second:
                    P.dma("sp", HFB[i3][:, :], s_hf[c * 128:(c + 1) * 128, :], reads=[b_hf], writes=[b_HFB[i3]])
                pCs = []
                for kt in range(2):
                    pC, bpC = next_ps()
                    P.op("pe", lambda h, pC=pC, kt=kt: h.matmul(pC[:, 0:257], KTM[:, c, kt * 128:(kt + 1) * 128], vp[:, :], start=True, stop=True),
                         reads=[b_KTM, b_VP[i2]], writes=[bpC])
                    pCs.append((pC, bpC))
                cp = CPB[i2]
                P.op("act", lambda h: h.activation(out=cp[:, :, :], in_=cst[:, :, :], func=AF.Copy, scale=DEC[:, hd, cc:cc + 1]),
                     reads=[bcst, b_scan], writes=[b_CPB[i2]])
                for kt in range(2):
                    pC, bpC = pCs[kt]
                    P.op("dve", lambda h, pC=pC, kt=kt: h.scalar_tensor_tensor(out=cst[:, kt, :], in0=cst[:, kt, :], scalar=DEC[:, hd, cc:cc + 1],
                         in1=pC[:, 0:257], op0=ALU.mult, op1=ALU.add), reads=[bpC, bcst, b_scan], writes=[bcst])
                pO, bpO = next_ps()
                P.op("pe", lambda h: h.matmul(pO[:, 0:257], wt[:, :], vp[:, :], start=True, stop=False), reads=[b_WT[i2], b_VP[i2]], writes=[bpO])
                for kt in range(2):
                    P.op("pe", lambda h, kt=kt: h.matmul(pO[:, 0:257], QT[:, kt, tk], cp[:, kt, :], start=False, stop=(kt == 1)),
                         reads=[b_QK, b_CPB[i2]], writes=[bpO])
                P.op("dve", lambda h: h.scalar_tensor_tensor(out=s1[:, 6:7], in0=pO[:, 256:257], scalar=-1.0, in1=s1[:, 1:2], op0=ALU.mult, op1=ALU.max),
                     reads=[bpO, b_SC1[i2]], writes=[b_SC1[i2]])
                P.op("dve", lambda h: h.tensor_tensor(out=s1[:, 2:3], in0=s1[:, 6:7], in1=pO[:, 256:257], op=ALU.max), reads=[bpO, b_SC1[i2]], writes=[b_SC1[i2]])
                P.op("dve", lambda h: h.reciprocal(out=s1[:, 3:4], in_=s1[:, 2:3]), reads=[b_SC1[i2]], writes=[b_SC1[i2]])
                hh = HH[i2]
                if not second:
                    P.op("dve", lambda h: h.tensor_scalar(out=hh[:, :], in0=pO[:, 0:256], scalar1=s1[:, 3:4], scalar2=None, op0=ALU.mult),
                         reads=[bpO, b_SC1[i2]], writes=[b_HH[i2]])
                    P.dma("sp", s_hf[c * 128:(c + 1) * 128, :], hh[:, :], reads=[b_HH[i2]], writes=[b_hf])
                else:
                    P.op("dve", lambda h: h.scalar_tensor_tensor(out=hh[:, :], in0=pO[:, 0:256], scalar=s1[:, 3:4], in1=HFB[i3][:, :],
                         op0=ALU.mult, op1=ALU.add), reads=[bpO, b_SC1[i2], b_HFB[i3]], writes=[b_HH[i2]])
                    bs = BST[i2]
                    P.op("dve", lambda h: h.bn_stats(out=bs[:, 0:6], in_=hh[:, :]), reads=[b_HH[i2]], writes=[b_BST[i2]])
                    P.op("dve", lambda h: h.bn_aggr(out=bs[:, 6:8], in_=bs[:, 0:6]), reads=[b_BST[i2]], writes=[b_BST[i2]])
                    P.op("act", lambda h: h.activation(out=s1[:, 4:5], in_=bs[:, 7:8], func=AF.Ln, bias=eps_sb[:, 0:1]), reads=[b_BST[i2], b_c], writes=[b_SC1[i2]])
                    P.op("act", lambda h: h.activation(out=s1[:, 5:6], in_=s1[:, 4:5], func=AF.Exp, scale=-0.5), reads=[b_SC1[i2]], writes=[b_SC1[i2]])
                    P.op("dve", lambda h: h.tensor_scalar(out=hh[:, :], in0=hh[:, :], scalar1=bs[:, 6:7], scalar2=s1[:, 5:6], op0=ALU.subtract, op1=ALU.mult),
                         reads=[b_HH[i2], b_BST[i2], b_SC1[i2]], writes=[b_HH[i2]])
                    P.op("dve", lambda h, head=head: h.tensor_tensor(out=hh[:, :], in0=hh[:, :], in1=MN[:, head * 256:(head + 1) * 256], op=ALU.mult),
                         reads=[b_HH[i2], b_MN], writes=[b_HH[i2]])
                    yb = YB[i2]
                    P.op("dve", lambda h: h.tensor_tensor(out=yb[:, :], in0=hh[:, :], in1=OB[i3][:, :], op=ALU.mult), reads=[b_HH[i2], b_OB[i3]], writes=[b_YB[i2]])
                    yo = YO[i2]
                    po, bpo = next_pst()
                    for ft in range(2):
                        P.op("pe", lambda h, ft=ft: h.transpose(out=po[:, ft, :], in_=yb[:, ft * 128:(ft + 1) * 128], identity=ident_bf[:, :]),
                             reads=[b_YB[i2], b_c], writes=[bpo])
                    P.op("act", lambda h: h.copy(out=yo[:, :, :], in_=po[:, 0:2, :]), reads=[bpo], writes=[b_YO[i2]])
                    P.dma("sp", syT_v[:, 8 + head * 2:8 + head * 2 + 2, c * 128:(c + 1) * 128], yo[:, :, :], reads=[b_YO[i2]], writes=[Buf('yst')])
            pend = [stageA(0, 0), stageA(1, 0)]
            for cc in range(NCH):
                nxt = []
                if cc + 1 < NCH:
                    nxt = [stageA(0, cc + 1), stageA(1, cc + 1)]
                for ctx in pend:
                    stageB(ctx)
                pend = nxt
        P.barrier()
        sc.close()

    if 1 in phases:
        dense_phase(1)
    if 2 in phases:
        attention_phase()
    if 3 in phases:
        scan_setup()
    if 4 in phases:
        mlstm_phase()
    if 5 in phases:
        dense_phase(3)
    P.emit()
    st.close()
    return nc


def _t5_bucket(rel):
    nb = 16
    max_exact = 8
    n = np.abs(rel)
    large = max_exact + (np.log(np.maximum(n, 1) / max_exact) / np.log(1024 / max_exact) * (nb - max_exact)).astype(np.int32)
    large = np.minimum(large, nb - 1)
    return np.where(rel > 0, nb, 0) + np.where(n < max_exact, n, large)


def host_consts(T):
    NCH = T // 128
    NQ = NCH * 8
    NB = NQ // 128
    i = np.arange(128)[:, None]
    j = np.arange(128)[None, :]
    cmask = np.zeros((128, 16, 128), np.float32)
    for bi in range(3):
        cmask[:, bi * 2 + 0, :] = (i >= j)
        cmask[:, bi * 2 + 1, :] = (i <= j)
        cmask[:, 6 + bi * 2 + 0, :] = (i >= j) & (i >= 64)
        cmask[:, 6 + bi * 2 + 1, :] = (i <= j) & (i < 64)
    cmask[:, 12, :] = (i <= j)
    cmask[:, 13, :] = (i >= j)
    cf32 = np.zeros((128, 3, 128), np.float32)
    cf32[:, 0, :] = (i <= j)
    cf32[127, 1, :] = 1.0
    cf32[:, 2, :] = (i == j)
    perm = np.zeros((NB, 128, NQ), np.float32)
    for q in range(NQ):
        c, hd = q // 8, q % 8
        cpp = c if hd < 4 else NCH - 1 - c
        perm[q // 128, q % 128, hd * NCH + cpp] = 1.0
    segm = np.zeros((128, NCH), np.float32)
    segm[:, ::16] = 1.0
    bidx = np.zeros((6, 128, 128), np.int64)
    for bi, d in enumerate((1, 4, 16)):
        bidx[bi * 2 + 0] = _t5_bucket((i - j - 64) * d)
        bidx[bi * 2 + 1] = _t5_bucket((i - j + 64) * d)
    return cmask, cf32, perm, segm, bidx


def host_inputs(inputs, T):
    f = lambda a: np.asarray(a, dtype=np.float32)
    common = {
        "w1in": _lay_ffn_in(f(inputs["ffn1_w_in"])[0]), "w1out": _lay_ffn_out(f(inputs["ffn1_w_out"])[0]),
        "w2in": _lay_ffn_in(f(inputs["ffn2_w_in"])[0]), "w2out": _lay_ffn_out(f(inputs["ffn2_w_out"])[0]),
        "wout": _lay_sq(f(inputs["w_out"])[0], KC), "wgate": _lay_sq(f(inputs["ple_w_gate"])[0], KC),
        "wproj": _lay_sq(f(inputs["ple_w_proj"])[0], 2),
    }
    w_in = f(inputs["w_in"])[0]
    A_, M_ = AW, MW
    fm_cols = np.concatenate([w_in[:, 0:2 * A_], w_in[:, 3 * A_:3 * A_ + 2 * M_]], axis=1)
    tm_cols = np.concatenate([w_in[:, 2 * A_:3 * A_], w_in[:, 3 * A_ + 2 * M_:3 * A_ + 4 * M_]], axis=1)
    t = fm_cols.reshape(KC, 128, 8, 4, 128).transpose(2, 1, 3, 0, 4)
    common["wina"] = np.ascontiguousarray(t).reshape(8, 128, 8192)
    common["winb"] = _lay_w_in_blocks(tm_cols, 512)
    common["wing"] = np.ascontiguousarray(w_in[:, 3 * A_ + 4 * M_:].reshape(KC, 128, NG).transpose(1, 0, 2)).reshape(128, KC * NG)
    common["nrm"] = np.concatenate([_lay_vec(f(inputs["ffn1_norm"])[0]), _lay_vec(f(inputs["mix_norm"])[0]),
                                    _lay_vec(f(inputs["ffn2_norm"])[0]), _lay_vec(f(inputs["ple_norm"])[0]),
                                    _lay_vec(f(inputs["final_norm"]))], axis=1)
    common["anrm"] = _lay_vec(f(inputs["attn_out_norm"])[0])
    common["mnrm"] = np.ascontiguousarray(np.broadcast_to(f(inputs["mlstm_out_norm"])[0][None, :], (128, MW)))
    common["bgr"] = np.ascontiguousarray(np.broadcast_to(f(inputs["b_gates"])[0][None, :], (128, NG)))
    cw = f(inputs["conv_w"])[0]
    common["cvw"] = np.ascontiguousarray(cw.reshape(3, 16, 128).transpose(2, 1, 0))
    common["cvb"] = _lay_vec(f(inputs["conv_b"])[0])
    cmask, cf32, perm, segm, bidx = host_consts(T)
    common["cmask"] = cmask; common["cf32"] = cf32; common["perm"] = perm; common["segm"] = segm
    rt = f(inputs["rel_table"])
    g = rt[bidx]
    common["relb"] = np.ascontiguousarray(g.transpose(3, 1, 0, 2))
    return common


_NC_CACHE = {}


def kernel(**inputs):
    T = 8192
    n = 8
    if T not in _NC_CACHE:
        _NC_CACHE[T] = build(T)
    nc = _NC_CACHE[T]
    common = host_inputs(inputs, T)
    xp = np.asarray(inputs["x_prompt"], np.float32)
    xs = np.asarray(inputs["x_sample"], np.float32)
    pp = np.asarray(inputs["p_prompt"], np.float32)[0]
    psm = np.asarray(inputs["p_sample"], np.float32)[0]
    S_CORES = (0, 4)
    P_CORES = (2, 6)
    toks = []
    for c in range(n):
        if c in S_CORES:
            i = S_CORES.index(c)
            toks.append((xs[i], psm[i], 0.0))
        elif c in P_CORES:
            i = P_CORES.index(c)
            sl = slice(i * 4, i * 4 + 4)
            toks.append((xp[sl].reshape(T, D), pp[sl].reshape(T, PLE), 1.0))
        else:
            toks.append((np.zeros((T, D), np.float32), np.zeros((T, PLE), np.float32), 1.0))
    in_maps = []
    for c in range(n):
        m = dict(common)
        m["xT"] = np.ascontiguousarray(toks[c][0].T)
        m["pT"] = np.ascontiguousarray(toks[c][1].T)
        m["sep"] = np.full((128, 1), toks[c][2], np.float32)
        in_maps.append(m)
    res = run_bass_kernel_spmd(nc, in_maps, core_ids=list(range(n)))
    outs = [np.ascontiguousarray(r["yT"].T) for r in res.results]
    y_sample = np.stack([outs[S_CORES[0]], outs[S_CORES[1]]], axis=0).astype(np.float32)
    y_prompt = np.concatenate([outs[P_CORES[0]].reshape(4, SEG, D), outs[P_CORES[1]].reshape(4, SEG, D)], axis=0).astype(np.float32)
    return (y_prompt, y_sample)
```

```python
import numpy as np
import concourse.bass as bass
import concourse.mybir as mybir
from concourse.bass_utils import run_bass_kernel_spmd

F32 = mybir.dt.float32
BF16 = mybir.dt.bfloat16
AF = mybir.ActivationFunctionType
ALU = mybir.AluOpType
AX = mybir.AxisListType

D = 2048
KC = 16
DFF = 5632
NJ = 44
AW = 1024
MW = 1024
NG = 16
INC = 3 * AW + 4 * MW + NG
PLE = 256
EPS = 1e-6
NT = 512
SEG = 2048
HALO = 1024


class Buf:
    __slots__ = ("name", "w", "r")

    def __init__(self, name):
        self.name = name
        self.w = None
        self.r = {}


class Prog:
    NDSLOT = 6

    def __init__(self, nc, stack):
        self.nc = nc
        self.stack = stack
        self.handles = {"pe": nc.tensor, "act": nc.scalar, "dve": nc.vector, "pool": nc.gpsimd, "sp": nc.sync}
        self.ops = {k: [] for k in self.handles}
        self.cnt = {k: 0 for k in self.handles}
        self.known = {k: {} for k in self.handles}
        self.sem = {k: stack.enter_context(nc.semaphore("sem_" + k)) for k in self.handles}
        self.dsem = {}
        self.dcnt = {}
        for q in ("sp", "pool", "act"):
            self.dsem[q] = [stack.enter_context(nc.semaphore("dsem_%s_%d" % (q, i))) for i in range(self.NDSLOT)]
            self.dcnt[q] = 0
        self.all_events = {}

    def _need(self, eng, dep):
        if dep is None:
            return
        sem, val, key = dep
        if eng == "pe" and key == "pe":
            return
        if self.known[eng].get(key, 0) >= val:
            return
        self.known[eng][key] = val
        self.ops[eng].append(("wait", sem, val))

    def _deps(self, eng, reads, writes):
        for b in reads:
            self._need(eng, b.w)
        for b in writes:
            self._need(eng, b.w)
            for d in list(b.r.values()):
                self._need(eng, d)

    def op(self, eng, fn, reads=(), writes=()):
        self._deps(eng, reads, writes)
        self.cnt[eng] += 1
        ev = (self.sem[eng], self.cnt[eng], eng)
        self.ops[eng].append(("op", fn, self.sem[eng], 1))
        self.all_events[eng] = ev
        for b in reads:
            b.r[eng] = ev
        for b in writes:
            b.w = ev
            b.r = {}
        return ev

    def dma(self, q, out, in_, reads=(), writes=(), **kw):
        k = self.dcnt[q]
        slot = k % self.NDSLOT
        sem = self.dsem[q][slot]
        key = "d_%s_%d" % (q, slot)
        if k >= self.NDSLOT:
            self._need(q, (sem, 16 * (k // self.NDSLOT), key))
        self._deps(q, reads, writes)
        self.dcnt[q] += 1
        ev = (sem, 16 * (k // self.NDSLOT + 1), key)
        self.ops[q].append(("op", lambda h: h.dma_start(out=out, in_=in_, **kw), sem, 16))
        self.all_events[key] = ev
        for b in reads:
            b.r[key] = ev
        for b in writes:
            b.w = ev
            b.r = {}
        return ev

    def barrier(self, engines=None):
        evs = list(self.all_events.values())
        for e in (engines or self.handles):
            for ev in evs:
                self._need(e, ev)

    def emit(self):
        nc = self.nc
        with nc.Block() as block:
            def run(name):
                def body(h):
                    for o in self.ops[name]:
                        if o[0] == "wait":
                            h.wait_ge(o[1], o[2])
                        else:
                            o[1](h).then_inc(o[2], o[3])
                return body
            block.tensor(run("pe"))
            block.scalar(run("act"))
            block.vector(run("dve"))
            block.gpsimd(run("pool"))
            block.sync(run("sp"))


def _lay_w_in_blocks(w, nblk_cols):
    K, N = w.shape
    kc = K // 128
    nb = N // nblk_cols
    return np.ascontiguousarray(w.reshape(kc, 128, nb, nblk_cols).transpose(2, 1, 0, 3)).reshape(nb, 128, kc * nblk_cols)


def _lay_ffn_in(w):
    t = w.reshape(KC, 128, 2, NJ // 2, 2, 128)
    t = t.transpose(3, 1, 4, 0, 2, 5)
    return np.ascontiguousarray(t).reshape(NJ // 2, 128, 2 * KC * 2 * 128)


def _lay_ffn_out(w):
    t = w.reshape(NJ, 128, KC, 128).transpose(2, 1, 0, 3)
    return np.ascontiguousarray(t).reshape(KC, 128, NJ * 128)


def _lay_sq(w, kc_in):
    t = w.reshape(kc_in, 128, 4, 4, 128).transpose(2, 1, 3, 0, 4)
    return np.ascontiguousarray(t).reshape(4, 128, 4 * kc_in * 128)


def _lay_vec(v):
    return np.ascontiguousarray(v.reshape(-1, 128).T)


DIL = (1, 4, 16)
import os as _os
ATT_PAIRS = int(_os.environ.get('ATT_PAIRS', '8'))
ATT_DILS = tuple(int(x) for x in _os.environ.get('ATT_DILS', '1,4,16').split(','))
NEGBIG = -1.0e30


def build(T, debug=False, phases=(1, 2, 3, 4, 5)):
    from contextlib import ExitStack
    NTILE = T // NT
    NSEG = T // SEG
    NCH = T // 128
    NQ = NCH * 8
    NB = NQ // 128
    nc = bass.Bass("TRN2", target_bir_lowering=False)
    st = ExitStack()
    P = Prog(nc, st)

    def din(name, shape, dt=F32):
        return nc.dram_tensor(name, list(shape), dt, kind="ExternalInput").ap()

    def dscr(name, shape, dt):
        kind = "ExternalOutput" if debug else "Internal"
        return nc.dram_tensor(name, list(shape), dt, kind=kind).ap()

    xT = din("xT", [D, T])
    pT = din("pT", [PLE, T])
    w1in = din("w1in", [NJ // 2, 128, 8192]); w1out = din("w1out", [KC, 128, NJ * 128])
    w2in = din("w2in", [NJ // 2, 128, 8192]); w2out = din("w2out", [KC, 128, NJ * 128])
    wina = din("wina", [8, 128, 8192])
    winb = din("winb", [6, 128, 8192])
    wing = din("wing", [128, KC * NG])
    wout = din("wout", [4, 128, 8192])
    wgate = din("wgate", [4, 128, 8192])
    wproj = din("wproj", [4, 128, 4 * 2 * 128])
    nrm = din("nrm", [128, 5 * KC])
    anrm = din("anrm", [128, 8])
    mnrm = din("mnrm", [128, MW])
    bgr = din("bgr", [128, NG])
    cvw = din("cvw", [128, 16, 3])
    cvb = din("cvb", [128, 16])
    sepi = din("sep", [128, 1])
    relb = din("relb", [16, 128, 6, 128])
    cmask = din("cmask", [128, 16, 128])
    cf32 = din("cf32", [128, 3, 128])
    perm = din("perm", [NB, 128, NQ])
    segm = din("segm", [128, NCH])
    yT = nc.dram_tensor("yT", [D, T], F32, kind="ExternalOutput").ap()
    s_w1in = dscr("s_w1in", [NJ // 2, 128, 8192], BF16); s_w1out = dscr("s_w1out", [KC, 128, NJ * 128], BF16)
    s_w2in = dscr("s_w2in", [NJ // 2, 128, 8192], BF16); s_w2out = dscr("s_w2out", [KC, 128, NJ * 128], BF16)
    s_wina = dscr("s_wina", [8, 128, 8192], BF16); s_winb = dscr("s_winb", [6, 128, 8192], BF16)
    s_wing = dscr("s_wing", [128, KC * NG], BF16)
    s_wout = dscr("s_wout", [4, 128, 8192], BF16); s_wgate = dscr("s_wgate", [4, 128, 8192], BF16)
    s_wproj = dscr("s_wproj", [4, 128, 1024], BF16)
    s_pT = dscr("s_pT", [PLE, T], BF16)
    s_x1 = dscr("s_x1", [D, T], F32)
    s_zfm = dscr("s_zfm", [4096, HALO + T + HALO], BF16)
    s_ztm = dscr("s_ztm", [HALO + T + HALO, 3072], BF16)
    s_yT = dscr("s_yT", [D, T], BF16)
    s_hf = dscr("s_hf", [T, 256], F32)

    wb = {}
    def cast_blocks(name, src, dst, nblk):
        for b in range(nblk):
            bf = Buf("%s_%d" % (name, b))
            P.dma("pool", dst[b], src[b], writes=[bf])
            wb[(name, b)] = bf
    bf = Buf("wing"); P.dma("pool", s_wing, wing, writes=[bf]); wb[("wing", 0)] = bf
    cast_blocks("w1in", w1in, s_w1in, NJ // 2)
    cast_blocks("w1out", w1out, s_w1out, KC)
    cast_blocks("wina", wina, s_wina, 8)
    cast_blocks("winb", winb, s_winb, 6)
    b_pT = Buf("s_pT")
    def deferred_casts():
        cast_blocks("wout", wout, s_wout, 4)
        cast_blocks("w2in", w2in, s_w2in, NJ // 2)
        cast_blocks("w2out", w2out, s_w2out, KC)
        cast_blocks("wgate", wgate, s_wgate, 4)
        cast_blocks("wproj", wproj, s_wproj, 4)
        P.dma("pool", s_pT, pT, writes=[b_pT])

    if _os.environ.get('BARRIER0'):
        P.barrier()
    def sb(name, shape, dt):
        return st.enter_context(nc.sbuf_tensor(name, list(shape), dt))
    b_c = Buf("consts")
    ones_bf = sb("ones_bf", [128, 128], BF16)
    P.op("dve", lambda h: h.memset(ones_bf[:, :], 1.0), writes=[b_c])
    ones_f = sb("ones_f", [128, 128], F32)
    P.op("dve", lambda h: h.memset(ones_f[:, :], 1.0), writes=[b_c])
    zeros_bf = sb("zeros_bf", [128, 3072], BF16)
    P.op("dve", lambda h: h.memset(zeros_bf[:, :], 0.0), writes=[b_c])
    eps_sb = sb("eps_sb", [128, 1], F32)
    P.op("dve", lambda h: h.memset(eps_sb[:, :], EPS), writes=[b_c])
    nrm_sb = sb("nrm_sb", [128, 5 * KC], F32); P.dma("sp", nrm_sb[:, :], nrm[:, :], writes=[b_c])
    anrm_sb = sb("anrm_sb", [128, 8], F32); P.dma("sp", anrm_sb[:, :], anrm[:, :], writes=[b_c])
    bgr_sb = sb("bgr_sb", [128, NG], F32); P.dma("sp", bgr_sb[:, :], bgr[:, :], writes=[b_c])
    cvw_sb = sb("cvw_sb", [128, 16, 3], F32); P.dma("sp", cvw_sb[:, :, :], cvw[:, :, :], writes=[b_c])
    cvb_sb = sb("cvb_sb", [128, 16], F32); P.dma("sp", cvb_sb[:, :], cvb[:, :], writes=[b_c])
    sep_sb = sb("sep_sb", [128, 1], F32); P.dma("sp", sep_sb[:, :], sepi[:, :], writes=[b_c])
    cf_sb = sb("cf_sb", [128, 3, 128], F32); P.dma("sp", cf_sb[:, :, :], cf32[:, :, :], writes=[b_c])
    cm_sb = sb("cm_sb", [128, 16, 128], F32); P.dma("sp", cm_sb[:, :, :], cmask[:, :, :], writes=[b_c])
    ident_bf = sb("ident_bf", [128, 128], BF16)
    P.op("dve", lambda h: h.tensor_copy(out=ident_bf[:, :], in_=cf_sb[:, 2, :]), reads=[b_c], writes=[b_c])
    cmc_bf = sb("cmc_bf", [128, 2, 128], BF16)
    P.op("dve", lambda h: h.tensor_copy(out=cmc_bf[:, :, :], in_=cm_sb[:, 12:14, :]), reads=[b_c], writes=[b_c])
    sepneg = sb("sepneg", [128, 1], F32)
    P.op("dve", lambda h: h.tensor_scalar(out=sepneg[:, :], in0=sep_sb[:, :], scalar1=NEGBIG, scalar2=None, op0=ALU.mult), reads=[b_c], writes=[b_c])
    w0s = sb("w0s", [128, 16], F32); w2s = sb("w2s", [128, 16], F32)
    P.op("dve", lambda h: h.tensor_scalar(out=w0s[:, :], in0=cvw_sb[:, :, 0], scalar1=sep_sb[:, 0:1], scalar2=None, op0=ALU.mult), reads=[b_c], writes=[b_c])
    P.op("dve", lambda h: h.tensor_scalar(out=w2s[:, :], in0=cvw_sb[:, :, 2], scalar1=sep_sb[:, 0:1], scalar2=None, op0=ALU.mult), reads=[b_c], writes=[b_c])
    wing_sb = sb("wing_sb", [128, KC * NG], BF16)
    P.dma("sp", wing_sb[:, :], s_wing, reads=[wb[("wing", 0)]], writes=[b_c])
    Gtm = sb("Gtm", [128, NCH, NG], F32); b_G = Buf("Gtm")
    Ug = sb("Ug", [128, NCH, 8], F32); Fg = sb("Fg", [128, NCH, 8], F32)
    NEGMC = sb("NEGMC", [128, 8, NCH], F32); DEC = sb("DEC", [128, 8, NCH], F32)
    b_scan = Buf("scan")

    b_pad = Buf("pads")
    pad_bufs = []
    zfm_v = s_zfm.rearrange("(c p) t -> p c t", p=128)
    padj = sb("padj", [128, 1], F32)
    def pad_dma(out, in_):
        pb = Buf("pad%d" % len(pad_bufs)); pad_bufs.append(pb)
        P.dma("sp", out, in_, reads=[b_c], writes=[pb])
    def emit_pads():
        for c in range(8, 32):
            pad_dma(zfm_v[:, c, 0:HALO], zeros_bf[:, 0:HALO])
            pad_dma(zfm_v[:, c, HALO + T:HALO + T + HALO], zeros_bf[:, 0:HALO])
        for r in range(HALO // 128):
            pad_dma(s_ztm[r * 128:(r + 1) * 128, :], zeros_bf[:, :])
            pad_dma(s_ztm[HALO + T + r * 128:HALO + T + (r + 1) * 128, :], zeros_bf[:, :])
        P.op("pool", lambda h: h.memset(padj[:, :], 0.0), reads=pad_bufs, writes=[b_pad])

    NPS = 6
    ps = [st.enter_context(nc.psum_tensor("ps%d" % i, [128, 512], F32)) for i in range(NPS)]
    b_ps = [Buf("ps%d" % i) for i in range(NPS)]
    psT = [st.enter_context(nc.psum_tensor("psT%d" % i, [128, 8, 128], BF16)) for i in range(2)]
    b_psT = [Buf("psT%d" % i) for i in range(2)]
    ps_rr = [0]; pst_rr = [0]
    ps_reserved = set()
    def next_ps():
        while True:
            i = ps_rr[0] % NPS
            ps_rr[0] += 1
            if i not in ps_reserved:
                return ps[i], b_ps[i]
    def next_pst():
        i = pst_rr[0] % 2
        pst_rr[0] += 1
        return psT[i], b_psT[i]

    xT_v = xT.rearrange("(kc p) t -> p kc t", p=128)
    x1_v = s_x1.rearrange("(kc p) t -> p kc t", p=128)
    yT_v = yT.rearrange("(kc p) t -> p kc t", p=128)
    syT_v = s_yT.rearrange("(kc p) t -> p kc t", p=128)
    spT_v = s_pT.rearrange("(kc p) t -> p kc t", p=128)
    b_x1 = [Buf("x1_%d" % i) for i in range(NTILE)]
    b_z = Buf("z")
    b_yT = Buf("s_yT")
    gen = [0]

    def dense_phase(which):
        dense = ExitStack()
        gen[0] += 1
        def dsb(name, shape, dt):
            return dense.enter_context(nc.sbuf_tensor("%s_g%d" % (name, gen[0]), list(shape), dt))
        X = dsb("X", [128, KC, NT], F32); b_X = Buf("X")
        HN = dsb("HN", [128, KC, NT], BF16); b_HN = Buf("HN"); b_HNm = [Buf("HNm%d" % m) for m in range(KC)]
        A = dsb("A", [128, NJ, NT], BF16); b_A = [Buf("A%d" % j) for j in range(NJ)]
        NWS = 3
        WS = [dsb("WS%d" % i, [128, 8192], BF16) for i in range(NWS)]; b_WS = [Buf("WS%d" % i) for i in range(NWS)]
        ws_rr = [0]
        RSTD = dsb("RSTD", [128, NT], F32); b_RSTD = Buf("RSTD")
        TMP = [dsb("TMP%d" % i, [128, NT], F32) for i in range(3)]; b_TMP = [Buf("TMP%d" % i) for i in range(3)]
        tmp_rr = [0]
        STG = [dsb("STG%d" % i, [128, NT], BF16) for i in range(6)]; b_STG = [Buf("STG%d" % i) for i in range(6)]
        stg_rr = [0]
        PTs = dsb("PTs", [128, 2, NT], BF16); b_PT = Buf("PT")
        YTs = dsb("YTs", [128, KC, NT], BF16); b_YT = Buf("YT")

        def load_w(src_ap, srcbuf, nelem=8192):
            i = ws_rr[0] % NWS
            ws_rr[0] += 1
            P.dma("sp", WS[i][:, 0:nelem], src_ap, reads=[srcbuf], writes=[b_WS[i]])
            return WS[i], b_WS[i]
        def next_tmp():
            i = tmp_rr[0] % 3
            tmp_rr[0] += 1
            return TMP[i], b_TMP[i]
        def next_stg():
            i = stg_rr[0] % 6
            stg_rr[0] += 1
            return STG[i], b_STG[i]

        def rms_stats():
            P.op("act", lambda h: h.activation(out=HN[:, :, :], in_=X[:, :, :], func=AF.Square), reads=[b_X], writes=[b_HN])
            pt, bp = next_ps()
            for kc in range(KC):
                P.op("pe", lambda h, kc=kc: h.matmul(pt[:, :], ones_bf[:, :], HN[:, kc, :], start=(kc == 0), stop=(kc == KC - 1)),
                     reads=[b_HN, b_c], writes=[bp])
            tt, bt = next_tmp()
            P.op("act", lambda h: h.activation(out=tt[:, :], in_=pt[:, :], func=AF.Sqrt, bias=eps_sb[:, 0:1], scale=1.0 / D),
                 reads=[bp, b_c], writes=[bt])
            P.op("dve", lambda h: h.reciprocal(out=RSTD[:, :], in_=tt[:, :]), reads=[bt], writes=[b_RSTD])

        stats = {}
        def fused_sq(m):
            if m == 0:
                stats["pt"], stats["bp"] = next_ps()
                ps_reserved.add(ps.index(stats["pt"]))
            P.op("act", lambda h, m=m: h.activation(out=HN[:, m, :], in_=X[:, m, :], func=AF.Square), reads=[b_X], writes=[b_HNm[m], b_HN])
            stats["pend"] = m
        def flush_sq():
            m = stats.get("pend")
            if m is None:
                return
            stats["pend"] = None
            pt, bp = stats["pt"], stats["bp"]
            P.op("pe", lambda h, m=m, pt=pt: h.matmul(pt[:, :], ones_bf[:, :], HN[:, m, :], start=(m == 0), stop=(m == KC - 1)),
                 reads=[b_HNm[m], b_c], writes=[bp])
        def rms_finish_stats():
            flush_sq()
            pt, bp = stats["pt"], stats["bp"]
            tt, bt = next_tmp()
            P.op("act", lambda h: h.activation(out=tt[:, :], in_=pt[:, :], func=AF.Sqrt, bias=eps_sb[:, 0:1], scale=1.0 / D),
                 reads=[bp, b_c], writes=[bt])
            ps_reserved.discard(ps.index(pt))
            P.op("dve", lambda h: h.reciprocal(out=RSTD[:, :], in_=tt[:, :]), reads=[bt], writes=[b_RSTD])

        def rmsnorm_to_HN(norm_idx, fused=False):
            if fused:
                rms_finish_stats()
            else:
                rms_stats()
            for kc in range(KC):
                gcol = nrm_sb[:, norm_idx * KC + kc: norm_idx * KC + kc + 1]
                P.op("dve", lambda h, kc=kc, gcol=gcol: h.scalar_tensor_tensor(
                    out=HN[:, kc, :], in0=X[:, kc, :], scalar=gcol, in1=RSTD[:, :], op0=ALU.mult, op1=ALU.mult),
                    reads=[b_X, b_RSTD, b_c], writes=[b_HN, b_HNm[kc]])

        def ffn(win_s, wout_s, win_name, wout_name):
            for jj in range(NJ // 2):
                W, bW = load_w(win_s[jj], wb[(win_name, jj)])
                Wv = W[:, :].rearrange("p (jl kc gu c) -> p jl kc gu c", jl=2, kc=KC, gu=2)
                def evac(j, pg, bpg, pu, bpu):
                    tt, bt = next_tmp()
                    P.op("act", lambda h, tt=tt, pg=pg: h.activation(out=tt[:, :], in_=pg[:, :], func=AF.Silu), reads=[bpg], writes=[bt])
                    P.op("dve", lambda h, tt=tt, pu=pu, j=j: h.tensor_tensor(out=A[:, j, :], in0=tt[:, :], in1=pu[:, :], op=ALU.mult),
                         reads=[bt, bpu], writes=[b_A[j]])
                if jj == 0:
                    banks = [(next_ps(), next_ps()) for jl in range(2)]
                    for kc in range(KC):
                        for jl in range(2):
                            (pg, bpg), (pu, bpu) = banks[jl]
                            P.op("pe", lambda h, kc=kc, jl=jl, pg=pg, Wv=Wv: h.matmul(pg[:, :], Wv[:, jl, kc, 0, :], HN[:, kc, :],
                                 start=(kc == 0), stop=(kc == KC - 1)), reads=[bW, b_HNm[kc]], writes=[bpg])
                            P.op("pe", lambda h, kc=kc, jl=jl, pu=pu, Wv=Wv: h.matmul(pu[:, :], Wv[:, jl, kc, 1, :], HN[:, kc, :],
                                 start=(kc == 0), stop=(kc == KC - 1)), reads=[bW, b_HNm[kc]], writes=[bpu])
                    for jl in range(2):
                        (pg, bpg), (pu, bpu) = banks[jl]
                        evac(jj * 2 + jl, pg, bpg, pu, bpu)
                else:
                    for jl in range(2):
                        pg, bpg = next_ps()
                        pu, bpu = next_ps()
                        for kc in range(KC):
                            P.op("pe", lambda h, kc=kc, jl=jl, pg=pg, Wv=Wv: h.matmul(pg[:, :], Wv[:, jl, kc, 0, :], HN[:, kc, :],
                                 start=(kc == 0), stop=(kc == KC - 1)), reads=[bW, b_HN], writes=[bpg])
                        for kc in range(KC):
                            P.op("pe", lambda h, kc=kc, jl=jl, pu=pu, Wv=Wv: h.matmul(pu[:, :], Wv[:, jl, kc, 1, :], HN[:, kc, :],
                                 start=(kc == 0), stop=(kc == KC - 1)), reads=[bW, b_HN], writes=[bpu])
                        evac(jj * 2 + jl, pg, bpg, pu, bpu)
            for m in range(KC):
                W, bW = load_w(wout_s[m][:, 0:NJ * 128], wb[(wout_name, m)], nelem=NJ * 128)
                Wv = W[:, 0:NJ * 128].rearrange("p (j c) -> p j c", j=NJ)
                po, bpo = next_ps()
                for j in range(NJ):
                    P.op("pe", lambda h, j=j, po=po, Wv=Wv: h.matmul(po[:, :], Wv[:, j, :], A[:, j, :], start=(j == 0), stop=(j == NJ - 1)),
                         reads=[bW, b_A[j]], writes=[bpo])
                flush_sq()
                P.op("dve", lambda h, m=m, po=po: h.scalar_tensor_tensor(out=X[:, m, :], in0=po[:, :], scalar=0.5, in1=X[:, m, :],
                     op0=ALU.mult, op1=ALU.add), reads=[bpo, b_X], writes=[b_X])
                fused_sq(m)

        if which == 1:
            P.dma("sp", X[:, :, :], xT_v[:, :, 0:NT], writes=[b_X])
            for it in range(NTILE):
                t0 = it * NT
                rmsnorm_to_HN(0)
                ffn(s_w1in, s_w1out, "w1in", "w1out")
                P.dma("sp", x1_v[:, :, t0:t0 + NT], X[:, :, :], reads=[b_X], writes=[b_x1[it]])
                rmsnorm_to_HN(1, fused=True)
                if it + 1 < NTILE:
                    P.dma("sp", X[:, :, :], xT_v[:, :, t0 + NT:t0 + 2 * NT], writes=[b_X])
                for cb in range(8):
                    W, bW = load_w(s_wina[cb], wb[("wina", cb)])
                    Wv = W[:, :].rearrange("p (cl kc c) -> p cl kc c", cl=4, kc=KC)
                    def zevac(c, pz, bpz):
                        sg, bsg = next_stg()
                        P.op("act", lambda h, sg=sg, pz=pz: h.copy(out=sg[:, :], in_=pz[:, :]), reads=[bpz], writes=[bsg])
                        P.dma("act", zfm_v[:, c, HALO + t0:HALO + t0 + NT], sg[:, :], reads=[bsg], writes=[Buf('zst')])
                    if cb == 0:
                        zb = [next_ps() for cl in range(4)]
                        for kc in range(KC):
                            for cl in range(4):
                                pz, bpz = zb[cl]
                                P.op("pe", lambda h, kc=kc, cl=cl, pz=pz, Wv=Wv: h.matmul(pz[:, :], Wv[:, cl, kc, :], HN[:, kc, :],
                                     start=(kc == 0), stop=(kc == KC - 1)), reads=[bW, b_HNm[kc]], writes=[bpz])
                        for cl in range(4):
                            zevac(cb * 4 + cl, *zb[cl])
                    else:
                        for cl in range(4):
                            pz, bpz = next_ps()
                            for kc in range(KC):
                                P.op("pe", lambda h, kc=kc, cl=cl, pz=pz, Wv=Wv: h.matmul(pz[:, :], Wv[:, cl, kc, :], HN[:, kc, :],
                                     start=(kc == 0), stop=(kc == KC - 1)), reads=[bW, b_HN], writes=[bpz])
                            zevac(cb * 4 + cl, pz, bpz)
                for cb in range(6):
                    W, bW = load_w(s_winb[cb], wb[("winb", cb)])
                    Wv = W[:, :].rearrange("p (kc c) -> p kc c", kc=KC)
                    for sub in range(NT // 128):
                        pz, bpz = next_ps()
                        for kc in range(KC):
                            P.op("pe", lambda h, kc=kc, sub=sub, pz=pz, Wv=Wv: h.matmul(pz[:, :], HN[:, kc, sub * 128:(sub + 1) * 128], Wv[:, kc, :],
                                 start=(kc == 0), stop=(kc == KC - 1)), reads=[bW, b_HN], writes=[bpz])
                        sg, bsg = next_stg()
                        P.op("act", lambda h, sg=sg, pz=pz: h.copy(out=sg[:, :], in_=pz[:, :]), reads=[bpz], writes=[bsg])
                        r0 = HALO + t0 + sub * 128
                        P.dma("act", s_ztm[r0:r0 + 128, cb * 512:(cb + 1) * 512], sg[:, :], reads=[bsg], writes=[Buf('zst')])
                for sub in range(NT // 128):
                    pz, bpz = next_ps()
                    for kc in range(KC):
                        P.op("pe", lambda h, kc=kc, sub=sub, pz=pz: h.matmul(pz[:, 0:NG], HN[:, kc, sub * 128:(sub + 1) * 128], wing_sb[:, kc * NG:(kc + 1) * NG],
                             start=(kc == 0), stop=(kc == KC - 1)), reads=[b_c, b_HN], writes=[bpz])
                    ch = (t0 + sub * 128) // 128
                    P.op("dve", lambda h, pz=pz, ch=ch: h.tensor_tensor(out=Gtm[:, ch, :], in0=pz[:, 0:NG], in1=bgr_sb[:, :], op=ALU.add),
                         reads=[bpz, b_c], writes=[b_G])
                if it == 0:
                    emit_pads()
                    deferred_casts()
        else:
            b_y = Buf("yout")
            P.dma("sp", YTs[:, :, :], syT_v[:, :, 0:NT], reads=[b_yT], writes=[b_YT])
            P.dma("sp", PTs[:, :, :], spT_v[:, :, 0:NT], reads=[b_pT], writes=[b_PT])
            for it in range(NTILE):
                t0 = it * NT
                P.dma("sp", X[:, :, :], x1_v[:, :, t0:t0 + NT], reads=[b_x1[it]], writes=[b_X])
                for mb in range(4):
                    W, bW = load_w(s_wout[mb], wb[("wout", mb)])
                    Wv = W[:, :].rearrange("p (ml kc c) -> p ml kc c", ml=4, kc=KC)
                    for ml in range(4):
                        m = mb * 4 + ml
                        po, bpo = next_ps()
                        for kc in range(KC):
                            P.op("pe", lambda h, kc=kc, ml=ml, po=po, Wv=Wv: h.matmul(po[:, :], Wv[:, ml, kc, :], YTs[:, kc, :],
                                 start=(kc == 0), stop=(kc == KC - 1)), reads=[bW, b_YT], writes=[bpo])
                        flush_sq()
                        P.op("dve", lambda h, m=m, po=po: h.tensor_tensor(out=X[:, m, :], in0=po[:, :], in1=X[:, m, :], op=ALU.add),
                             reads=[bpo, b_X], writes=[b_X])
                        fused_sq(m)
                if it + 1 < NTILE:
                    P.dma("sp", YTs[:, :, :], syT_v[:, :, t0 + NT:t0 + 2 * NT], reads=[b_yT], writes=[b_YT])
                rmsnorm_to_HN(2, fused=True)
                ffn(s_w2in, s_w2out, "w2in", "w2out")
                rmsnorm_to_HN(3, fused=True)
                for mb in range(4):
                    W, bW = load_w(s_wgate[mb], wb[("wgate", mb)])
                    Wv = W[:, :].rearrange("p (ml kc c) -> p ml kc c", ml=4, kc=KC)
                    W2, bW2 = load_w(s_wproj[mb], wb[("wproj", mb)], nelem=1024)
                    W2v = W2[:, 0:1024].rearrange("p (ml kc c) -> p ml kc c", ml=4, kc=2)
                    for ml in range(4):
                        m = mb * 4 + ml
                        pg, bpg = next_ps()
                        pp, bpp = next_ps()
                        for kc in range(KC):
                            P.op("pe", lambda h, kc=kc, ml=ml, pg=pg, Wv=Wv: h.matmul(pg[:, :], Wv[:, ml, kc, :], HN[:, kc, :],
                                 start=(kc == 0), stop=(kc == KC - 1)), reads=[bW, b_HN], writes=[bpg])
                        for kc in range(2):
                            P.op("pe", lambda h, kc=kc, ml=ml, pp=pp, W2v=W2v: h.matmul(pp[:, :], W2v[:, ml, kc, :], PTs[:, kc, :],
                                 start=(kc == 0), stop=(kc == 1)), reads=[bW2, b_PT], writes=[bpp])
                        tt, bt = next_tmp()
                        P.op("act", lambda h, tt=tt, pg=pg: h.activation(out=tt[:, :], in_=pg[:, :], func=AF.Sigmoid), reads=[bpg], writes=[bt])
                        t2, bt2 = next_tmp()
                        P.op("dve", lambda h, tt=tt, t2=t2, pp=pp: h.tensor_tensor(out=t2[:, :], in0=tt[:, :], in1=pp[:, :], op=ALU.mult),
                             reads=[bt, bpp], writes=[bt2])
                        P.op("dve", lambda h, m=m, t2=t2: h.tensor_tensor(out=X[:, m, :], in0=t2[:, :], in1=X[:, m, :], op=ALU.add),
                             reads=[bt2, b_X], writes=[b_X])
                if it + 1 < NTILE:
                    P.dma("sp", PTs[:, :, :], spT_v[:, :, t0 + NT:t0 + 2 * NT], reads=[b_pT], writes=[b_PT])
                rms_stats()
                for kc in range(KC):
                    gcol = nrm_sb[:, 4 * KC + kc: 4 * KC + kc + 1]
                    P.op("dve", lambda h, kc=kc, gcol=gcol: h.scalar_tensor_tensor(
                        out=X[:, kc, :], in0=X[:, kc, :], scalar=gcol, in1=RSTD[:, :], op0=ALU.mult, op1=ALU.mult),
                        reads=[b_X, b_RSTD, b_c], writes=[b_X])
                P.dma("sp", yT_v[:, :, t0:t0 + NT], X[:, :, :], reads=[b_X], writes=[b_y])
        P.barrier()
        dense.close()

    def attention_phase():
        sc = ExitStack()
        def asb(name, shape, dt):
            return sc.enter_context(nc.sbuf_tensor("att_" + name, list(shape), dt))
        NTL = {1: 17, 4: 5, 16: 2}
        QT = [asb("QT%d" % k, [128, SEG], BF16) for k in range(2)]
        KT = [asb("KT%d" % k, [128, HALO + SEG + HALO], BF16) for k in range(2)]
        VT = [{d: asb("VT%d_%d" % (d, k), [128, d * NTL[d], 128], BF16) for d in DIL} for k in range(2)]
        b_LD = [dict() for k in range(2)]
        RB = [asb("RB%d" % k, [128, 6, 128], F32) for k in range(2)]; b_RB = [Buf("RB%d" % k) for k in range(2)]
        EXPB = asb("EXPB", [128, 6, 128], F32); DIFF = asb("DIFF", [128, 6, 128], F32); b_EW = Buf("EW")
        EBI = [asb("EBI%d" % k, [128, 6, 128], BF16) for k in range(2)]
        EBB = [asb("EBB%d" % k, [128, 6, 128], BF16) for k in range(2)]
        EBX = [asb("EBX%d" % k, [128, 6, 128], BF16) for k in range(2)]
        b_EB = [Buf("EB%d" % k) for k in range(2)]
        NET = 4
        ET = [asb("ET%d" % i, [128, 2, 128], BF16) for i in range(NET)]; b_ET = [Buf("ET%d" % i) for i in range(NET)]
        PTt = [asb("PTt%d" % i, [128, 2, 128], BF16) for i in range(NET)]; b_PTt = [Buf("PTt%d" % i) for i in range(NET)]
        ACCN = asb("ACCN", [128, SEG], F32); ACCD = asb("ACCD", [128, SEG], F32)
        b_ACC = [Buf("ACC0"), Buf("ACC1")]
        ATT = asb("ATT", [128, 8, SEG], F32); b_ATT = Buf("ATT")
        SQ = asb("SQ", [128, NT], BF16); b_SQ = Buf("SQ")
        R1 = asb("R1", [128, NT], F32); R2 = asb("R2", [128, NT], F32); b_R = Buf("R")
        YS = [asb("YS%d" % i, [128, NT], BF16) for i in range(2)]; b_YS = [Buf("YS%d" % i) for i in range(2)]
        rr = {"et": 0, "ys": 0, "ld": 0, "eb": 0}
        LAG = 2

        for seg in range(NSEG):
            sb0 = seg * SEG
            for pr in range(ATT_PAIRS):
                k = rr["ld"] % 2; rr["ld"] += 1
                QTk, KTk, VTk, bL = QT[k], KT[k], VT[k], b_LD[k]
                if not bL:
                    bL["q"] = Buf("ldq%d" % k); bL["k"] = Buf("ldk%d" % k)
                    for d_ in DIL:
                        for r_ in range(d_):
                            bL[(d_, r_)] = Buf("ldv%d_%d_%d" % (k, d_, r_))
                P.dma("sp", QTk[:, :], zfm_v[:, pr, HALO + sb0:HALO + sb0 + SEG], reads=[b_z], writes=[bL["q"]])
                P.dma("sp", KTk[:, :], zfm_v[:, 8 + pr, sb0:sb0 + SEG + 2 * HALO], reads=[b_z, b_pad], writes=[bL["k"]])
                for d in DIL:
                    for r in range(d):
                        base_row = HALO + sb0 - 64 * d + r
                        src = s_ztm[base_row:base_row + d * (128 * NTL[d] - 1) + 1:d, pr * 128:(pr + 1) * 128]
                        P.dma("sp", VTk[d][:, r * NTL[d]:(r + 1) * NTL[d], :], src.rearrange("(n p) c -> p n c", p=128),
                              reads=[b_z, b_pad], writes=[bL[(d, r)]])
                for hh in range(2):
                    head = pr * 2 + hh
                    hs = slice(hh * 64, hh * 64 + 64)
                    bA = b_ACC[hh]
                    e = rr["eb"] % 2; rr["eb"] += 1
                    RBe, EBIe, EBBe, EBXe, bE = RB[e], EBI[e], EBB[e], EBX[e], b_EB[e]
                    P.dma("sp", RBe[:, :, :], relb[head], writes=[b_RB[e]])
                    P.op("act", lambda h, RBe=RBe: h.activation(out=EXPB[:, :, :], in_=RBe[:, :, :], func=AF.Exp), reads=[b_RB[e]], writes=[b_EW])
                    P.op("dve", lambda h, EBIe=EBIe: h.tensor_tensor(out=EBIe[:, :, :], in0=EXPB[:, :, :], in1=cm_sb[:, 0:6, :], op=ALU.mult), reads=[b_EW, b_c], writes=[bE])
                    P.op("dve", lambda h, EBBe=EBBe: h.tensor_tensor(out=EBBe[:, :, :], in0=EXPB[:, :, :], in1=cm_sb[:, 6:12, :], op=ALU.mult), reads=[b_EW, b_c], writes=[bE])
                    P.op("dve", lambda h, EBBe=EBBe, EBIe=EBIe: h.tensor_tensor(out=DIFF[:, :, :], in0=EBBe[:, :, :], in1=EBIe[:, :, :], op=ALU.subtract), reads=[bE], writes=[b_EW])
                    P.op("dve", lambda h, EBXe=EBXe, EBIe=EBIe: h.scalar_tensor_tensor(out=EBXe[:, :, :], in0=DIFF[:, :, :], scalar=sep_sb[:, 0:1], in1=EBIe[:, :, :],
                         op0=ALU.mult, op1=ALU.add), reads=[b_EW, bE, b_c], writes=[bE])
                    items = []
                    for bi, d in enumerate(DIL):
                        if d not in ATT_DILS:
                            continue
                        nqt = (SEG // d) // 128
                        tl = [(r, n) for r in range(d) for n in range(nqt)]
                        for b0 in range(0, len(tl), 2):
                            for qi in range(2):
                                r, n = tl[b0 + qi]
                                items.append(dict(bi=bi, d=d, r=r, n=n, nqt=nqt, qi=qi, first=tl[b0]))
                    state = {}
                    def stage1(it):
                        d, r, n, bi, nqt = it["d"], it["r"], it["n"], it["bi"], it["nqt"]
                        qcols = QTk[hs, d * 128 * n + r: d * 128 * n + r + d * 127 + 1: d]
                        pss, bps = next_ps()
                        for tau in range(2):
                            kstart = HALO + d * (-64 + 128 * (n + tau)) + r
                            kcols = KTk[hs, kstart:kstart + d * 127 + 1:d]
                            P.op("pe", lambda h, pss=pss, tau=tau, kcols=kcols, qcols=qcols: h.matmul(
                                pss[:, tau * 128:(tau + 1) * 128], kcols, qcols, start=True, stop=True),
                                reads=[bL["q"], bL["k"]], writes=[bps])
                        ei = rr["et"] % NET; rr["et"] += 1
                        E = ET[ei]; PT_ = PTt[ei]
                        P.op("act", lambda h, E=E, pss=pss: h.activation(out=E[:, :, :], in_=pss[:, 0:256].rearrange("p (a b) -> p a b", a=2),
                             func=AF.Exp, scale=0.125), reads=[bps], writes=[b_ET[ei]])
                        ebs = []
                        for tau in range(2):
                            if tau == 0 and n == 0:
                                ebs.append(EBBe if seg == 0 else EBXe)
                            elif tau == 1 and n == nqt - 1:
                                ebs.append(EBBe if seg == NSEG - 1 else EBXe)
                            else:
                                ebs.append(EBIe)
                        if ebs[0] is ebs[1]:
                            eb = ebs[0]
                            P.op("dve", lambda h, PT_=PT_, E=E, eb=eb, bi=bi: h.tensor_tensor(
                                out=PT_[:, :, :], in0=E[:, :, :], in1=eb[:, bi * 2:bi * 2 + 2, :], op=ALU.mult),
                                reads=[b_ET[ei], bE], writes=[b_PTt[ei]])
                        else:
                            for tau in range(2):
                                eb = ebs[tau]
                                P.op("dve", lambda h, PT_=PT_, E=E, eb=eb, tau=tau, bi=bi: h.tensor_tensor(
                                    out=PT_[:, tau, :], in0=E[:, tau, :], in1=eb[:, bi * 2 + tau, :], op=ALU.mult),
                                    reads=[b_ET[ei], bE], writes=[b_PTt[ei]])
                        it["PT"] = PT_; it["bPT"] = b_PTt[ei]
                    def stage2(it):
                        d, r, n, bi, qi = it["d"], it["r"], it["n"], it["bi"], it["qi"]
                        if qi == 0:
                            state["pnd"], state["bpnd"] = next_ps()
                        pnd, bpnd = state["pnd"], state["bpnd"]
                        PT_ = it["PT"]
                        for tau in range(2):
                            vt = VTk[d][:, r * NTL[d] + n + tau, :]
                            P.op("pe", lambda h, pnd=pnd, qi=qi, vt=vt, PT_=PT_, tau=tau: h.matmul(
                                pnd[:, qi * 128:(qi + 1) * 128], vt, PT_[:, tau, :], start=(tau == 0), stop=(tau == 1)),
                                reads=[bL[(d, r)], it["bPT"]], writes=[bpnd])
                        for tau in range(2):
                            P.op("pe", lambda h, pnd=pnd, qi=qi, PT_=PT_, tau=tau: h.matmul(
                                pnd[:, 256 + qi * 128:256 + (qi + 1) * 128], ones_bf[:, :], PT_[:, tau, :], start=(tau == 0), stop=(tau == 1)),
                                reads=[b_c, it["bPT"]], writes=[bpnd])
                        if qi == 1:
                            r0, n0 = it["first"]
                            if d == 1:
                                oN = ACCN[hs, 128 * n0:128 * n0 + 256]; oD = ACCD[hs, 128 * n0:128 * n0 + 256]
                                iN = pnd[hs, 0:256]; iD = pnd[hs, 256:512]
                            elif d == 4:
                                t_0 = r0 + 4 * 128 * n0
                                oN = ACCN[hs, t_0:t_0 + 4 * 255 + 1:4]; oD = ACCD[hs, t_0:t_0 + 4 * 255 + 1:4]
                                iN = pnd[hs, 0:256]; iD = pnd[hs, 256:512]
                            else:
                                oN = ACCN[hs, :].rearrange("p (j r) -> p r j", r=16)[:, r0:r0 + 2, :]
                                oD = ACCD[hs, :].rearrange("p (j r) -> p r j", r=16)[:, r0:r0 + 2, :]
                                iN = pnd[hs, 0:256].rearrange("p (a b) -> p a b", a=2); iD = pnd[hs, 256:512].rearrange("p (a b) -> p a b", a=2)
                            if bi == 0:
                                P.op("act", lambda h, oN=oN, iN=iN: h.copy(out=oN, in_=iN), reads=[bpnd], writes=[bA])
                                P.op("act", lambda h, oD=oD, iD=iD: h.copy(out=oD, in_=iD), reads=[bpnd], writes=[bA])
                            else:
                                P.op("dve", lambda h, oN=oN, iN=iN: h.tensor_tensor(out=oN, in0=iN, in1=oN, op=ALU.add), reads=[bpnd, bA], writes=[bA])
                                P.op("dve", lambda h, oD=oD, iD=iD: h.tensor_tensor(out=oD, in0=iD, in1=oD, op=ALU.add), reads=[bpnd, bA], writes=[bA])
                    for idx in range(len(items) + LAG):
                        if idx < len(items):
                            stage1(items[idx])
                        if idx - LAG >= 0:
                            stage2(items[idx - LAG])
                    P.op("dve", lambda h, hs=hs: h.reciprocal(out=ACCD[hs, :], in_=ACCD[hs, :]), reads=[bA], writes=[bA])
                    P.op("dve", lambda h, hs=hs, pr=pr: h.tensor_tensor(out=ATT[hs, pr, :], in0=ACCN[hs, :], in1=ACCD[hs, :], op=ALU.mult),
                         reads=[bA], writes=[b_ATT])
            for blk in range(SEG // NT):
                cs = slice(blk * NT, (blk + 1) * NT)
                pt, bp = next_ps()
                for pr in range(8):
                    P.op("act", lambda h, pr=pr, cs=cs: h.activation(out=SQ[:, :], in_=ATT[:, pr, cs], func=AF.Square), reads=[b_ATT], writes=[b_SQ])
                    P.op("pe", lambda h, pr=pr, pt=pt: h.matmul(pt[:, :], ones_bf[:, :], SQ[:, :], start=(pr == 0), stop=(pr == 7)),
                         reads=[b_SQ, b_c], writes=[bp])
                P.op("act", lambda h, pt=pt: h.activation(out=R1[:, :], in_=pt[:, :], func=AF.Sqrt, bias=eps_sb[:, 0:1], scale=1.0 / AW),
                     reads=[bp, b_c], writes=[b_R])
                P.op("dve", lambda h: h.reciprocal(out=R2[:, :], in_=R1[:, :]), reads=[b_R], writes=[b_R])
                for pr in range(8):
                    yi = rr["ys"] % 2; rr["ys"] += 1
                    P.op("dve", lambda h, pr=pr, cs=cs, yi=yi: h.scalar_tensor_tensor(out=YS[yi][:, :], in0=ATT[:, pr, cs], scalar=anrm_sb[:, pr:pr + 1],
                         in1=R2[:, :], op0=ALU.mult, op1=ALU.mult), reads=[b_ATT, b_R, b_c], writes=[b_YS[yi]])
                    P.dma("sp", syT_v[:, pr, sb0 + blk * NT:sb0 + (blk + 1) * NT], YS[yi][:, :], reads=[b_YS[yi]], writes=[Buf('yst')])
        P.barrier()
        sc.close()

    def scan_setup():
        sc = ExitStack()
        def ssb(name, shape, dt):
            return sc.enter_context(nc.sbuf_tensor("scn_" + name, list(shape), dt))
        LI = ssb("LI", [128, NCH, 8], F32); FP = ssb("FP", [128, NCH, 8], F32); LF = ssb("LF", [128, NCH, 8], F32)
        PL = ssb("PL", [128, NCH, 8], F32); TOT = ssb("TOT", [128, NCH, 8], F32)
        OFFI = ssb("OFFI", [128, 8, NCH], F32); ONESC = ssb("ONESC", [128, NCH], F32)
        UT = ssb("UT", [128, NB, 128], F32); UMC = ssb("UMC", [128, NB], F32)
        PRM = ssb("PRM", [128, NQ], F32); PRS = ssb("PRS", [128, NQ], F32)
        UMR = ssb("UMR", [128, 8, NCH], F32); RR = ssb("RR", [128, NCH], F32)
        MC = ssb("MC", [128, 8, NCH], F32); MSH = ssb("MSH", [128, 8, NCH], F32); SEGM = ssb("SEGM", [128, NCH], F32)
        b = Buf("scanwork")
        gv = Gtm[:, :, :].rearrange("p c (g h) -> p c g h", g=4)
        P.dma("sp", SEGM[:, :], segm[:, :], writes=[b])
        P.op("dve", lambda h: h.memset(ONESC[:, :], 1.0), writes=[b])
        P.op("dve", lambda h: h.tensor_copy(out=LI[:, :, 0:4], in_=gv[:, :, 0, :]), reads=[b_G], writes=[b])
        P.op("dve", lambda h: h.tensor_copy(out=LI[:, :, 4:8], in_=gv[:, :, 2, :]), reads=[b_G], writes=[b])
        P.op("dve", lambda h: h.tensor_copy(out=FP[:, :, 0:4], in_=gv[:, :, 1, :]), reads=[b_G], writes=[b])
        P.op("dve", lambda h: h.tensor_copy(out=FP[:, :, 4:8], in_=gv[:, :, 3, :]), reads=[b_G], writes=[b])
        P.op("act", lambda h: h.activation(out=LF[:, :, :], in_=FP[:, :, :], func=AF.Exp, scale=-1.0), reads=[b], writes=[b])
        P.op("dve", lambda h: h.tensor_scalar(out=LF[:, :, :], in0=LF[:, :, :], scalar1=1.0, scalar2=None, op0=ALU.add), reads=[b], writes=[b])
        P.op("act", lambda h: h.activation(out=LF[:, :, :], in_=LF[:, :, :], func=AF.Ln), reads=[b], writes=[b])
        P.op("dve", lambda h: h.tensor_scalar(out=LF[:, :, :], in0=LF[:, :, :], scalar1=-1.0, scalar2=None, op0=ALU.mult), reads=[b], writes=[b])
        pt, bp = next_ps()
        P.op("pe", lambda h: h.matmul(pt[:, 0:NQ], cf_sb[:, 0, :], LF[:, :, :].rearrange("p c h -> p (c h)"), start=True, stop=True),
             reads=[b, b_c], writes=[bp])
        P.op("dve", lambda h: h.tensor_copy(out=PL[:, :, :].rearrange("p c h -> p (c h)"), in_=pt[:, 0:NQ]), reads=[bp], writes=[b])
        pt2, bp2 = next_ps()
        P.op("pe", lambda h: h.matmul(pt2[:, 0:NQ], cf_sb[:, 1, :], PL[:, :, :].rearrange("p c h -> p (c h)"), start=True, stop=True),
             reads=[b, b_c], writes=[bp2])
        P.op("dve", lambda h: h.tensor_copy(out=TOT[:, :, :].rearrange("p c h -> p (c h)"), in_=pt2[:, 0:NQ]), reads=[bp2], writes=[b])
        for hd in range(8):
            P.op("dve", lambda h, hd=hd: h.tensor_tensor_scan(out=OFFI[:, hd, :], data0=ONESC[:, :], data1=TOT[:, :, hd], initial=0.0,
                 op0=ALU.mult, op1=ALU.add), reads=[b], writes=[b])
        P.op("dve", lambda h: h.tensor_tensor(out=PL[:, :, :], in0=PL[:, :, :], in1=OFFI[:, :, :].rearrange("p h c -> p c h"), op=ALU.add), reads=[b], writes=[b])
        P.op("dve", lambda h: h.tensor_tensor(out=PL[:, :, :], in0=PL[:, :, :], in1=TOT[:, :, :], op=ALU.subtract), reads=[b], writes=[b])
        P.op("dve", lambda h: h.tensor_copy(out=Fg[:, :, 0:4], in_=PL[:, :, 0:4]), reads=[b], writes=[b_scan])
        P.op("dve", lambda h: h.tensor_tensor(out=Fg[:, :, 4:8], in0=LF[:, :, 4:8], in1=PL[:, :, 4:8], op=ALU.subtract), reads=[b], writes=[b_scan])
        P.op("dve", lambda h: h.tensor_tensor(out=Ug[:, :, :], in0=LI[:, :, :], in1=Fg[:, :, :], op=ALU.subtract), reads=[b, b_scan], writes=[b_scan])
        uflat = Ug[:, :, :].rearrange("p c h -> p (c h)")
        for nb in range(NB):
            ptt, bpt = next_ps()
            P.op("pe", lambda h, nb=nb, ptt=ptt: h.transpose(out=ptt[:, 0:128], in_=uflat[:, nb * 128:(nb + 1) * 128], identity=cf_sb[:, 2, :]),
                 reads=[b_scan, b_c], writes=[bpt])
            P.op("dve", lambda h, nb=nb, ptt=ptt: h.tensor_reduce(out=UMC[:, nb:nb + 1], in_=ptt[:, 0:128], axis=AX.X, op=ALU.max), reads=[bpt], writes=[b])
        pu, bpu = next_ps()
        for nb in range(NB):
            P.dma("sp", PRM[:, :], perm[nb], writes=[b])
            P.op("dve", lambda h, nb=nb: h.tensor_scalar(out=PRS[:, :], in0=PRM[:, :], scalar1=UMC[:, nb:nb + 1], scalar2=None, op0=ALU.mult), reads=[b], writes=[b])
            P.op("pe", lambda h, nb=nb, pu=pu: h.matmul(pu[:, 0:NQ], ones_f[:, :], PRS[:, :], start=(nb == 0), stop=(nb == NB - 1)), reads=[b, b_c], writes=[bpu])
        P.op("dve", lambda h: h.tensor_copy(out=UMR[:, :, :].rearrange("p h c -> p (h c)"), in_=pu[:, 0:NQ]), reads=[bpu], writes=[b])
        P.op("dve", lambda h: h.tensor_scalar(out=RR[:, :], in0=SEGM[:, :], scalar1=sepneg[:, 0:1], scalar2=None, op0=ALU.mult), reads=[b, b_c], writes=[b])
        for hd in range(8):
            P.op("dve", lambda h, hd=hd: h.tensor_tensor_scan(out=MC[:, hd, :], data0=RR[:, :], data1=UMR[:, hd, :], initial=NEGBIG,
                 op0=ALU.add, op1=ALU.max), reads=[b], writes=[b])
        P.op("dve", lambda h: h.tensor_scalar(out=NEGMC[:, :, :], in0=MC[:, :, :], scalar1=-1.0, scalar2=None, op0=ALU.mult), reads=[b], writes=[b_scan])
        P.op("dve", lambda h: h.memset(MSH[:, :, :], NEGBIG), writes=[b])
        P.op("dve", lambda h: h.tensor_copy(out=MSH[:, :, 1:NCH], in_=MC[:, :, 0:NCH - 1]), reads=[b], writes=[b])
        P.op("dve", lambda h: h.tensor_tensor(out=MSH[:, :, :], in0=MSH[:, :, :], in1=RR[:, :].unsqueeze(1).to_broadcast([128, 8, NCH]), op=ALU.add), reads=[b], writes=[b])
        P.op("dve", lambda h: h.tensor_tensor(out=MSH[:, :, :], in0=MSH[:, :, :], in1=MC[:, :, :], op=ALU.subtract), reads=[b], writes=[b])
        P.op("dve", lambda h: h.tensor_scalar(out=MSH[:, :, :], in0=MSH[:, :, :], scalar1=-80.0, scalar2=None, op0=ALU.max), reads=[b], writes=[b])
        P.op("act", lambda h: h.activation(out=DEC[:, :, :], in_=MSH[:, :, :], func=AF.Exp), reads=[b], writes=[b_scan])
        P.barrier()
        sc.close()

    def mlstm_phase():
        sc = ExitStack()
        def msb(name, shape, dt):
            return sc.enter_context(nc.sbuf_tensor("ml_" + name, list(shape), dt))
        QT = msb("QT", [128, 2, T], BF16); KT = msb("KT", [128, 2, T], BF16); b_QK = Buf("QK")
        KTM = msb("KTM", [128, NCH, 256], BF16); b_KTM = Buf("KTM")
        XPs = [msb("XP%d" % i, [128, SEG + 2], BF16) for i in range(2)]; b_XPs = [Buf("XP%d" % i) for i in range(2)]
        YCs = [msb("YC%d" % i, [128, SEG], F32) for i in range(2)]; b_YCs = [Buf("YC%d" % i) for i in range(2)]
        xp_rr = [0]
        TC = msb("TC", [128, 1], F32); b_TC = Buf("TC")
        NR = 6
        VB = [msb("VB%d" % i, [128, 256], BF16) for i in range(NR)]; b_VB = [Buf("VB%d" % i) for i in range(NR)]
        OB = [msb("OB%d" % i, [128, 256], BF16) for i in range(NR)]; b_OB = [Buf("OB%d" % i) for i in range(NR)]
        HFB = [msb("HFB%d" % i, [128, 256], F32) for i in range(NR)]; b_HFB = [Buf("HFB%d" % i) for i in range(NR)]
        NA = 4
        VP = [msb("VP%d" % i, [128, 257], BF16) for i in range(NA)]; b_VP = [Buf("VP%d" % i) for i in range(NA)]
        WT = [msb("WT%d" % i, [128, 128], BF16) for i in range(NA)]; b_WT = [Buf("WT%d" % i) for i in range(NA)]
        CST = [msb("CST%d" % i, [128, 2, 257], F32) for i in range(2)]; b_CST = [Buf("CST%d" % i) for i in range(2)]
        CPB = [msb("CPB%d" % i, [128, 2, 257], BF16) for i in range(NA)]; b_CPB = [Buf("CPB%d" % i) for i in range(NA)]
        SC1 = [msb("SC1_%d" % i, [128, 8], F32) for i in range(NA)]; b_SC1 = [Buf("SC1_%d" % i) for i in range(NA)]
        HH = [msb("HH%d" % i, [128, 256], F32) for i in range(NA)]; b_HH = [Buf("HH%d" % i) for i in range(NA)]
        SG = [msb("SG%d" % i, [128, 256], F32) for i in range(NA)]; b_SG = [Buf("SG%d" % i) for i in range(NA)]
        BST = [msb("BST%d" % i, [128, 8], F32) for i in range(NA)]; b_BST = [Buf("BST%d" % i) for i in range(NA)]
        YB = [msb("YB%d" % i, [128, 256], BF16) for i in range(NA)]; b_YB = [Buf("YB%d" % i) for i in range(NA)]
        YO = [msb("YO%d" % i, [128, 2, 128], BF16) for i in range(NA)]; b_YO = [Buf("YO%d" % i) for i in range(NA)]
        MN = msb("MN", [128, MW], F32); b_MN = Buf("MN")
        P.dma("sp", MN[:, :], mnrm[:, :], writes=[b_MN])
        b_hf = Buf("s_hf")
        rr = [0]

        for head in range(4):
            for qk in range(2):
                dst = QT if qk == 0 else KT
                for ft in range(2):
                    cidx = qk * 8 + head * 2 + ft
                    zc = 16 + qk * 8 + head * 2 + ft
                    for seg in range(NSEG):
                        sb0 = seg * SEG
                        XP = XPs[xp_rr[0] % 2]; b_XP = b_XPs[xp_rr[0] % 2]
                        YC = YCs[xp_rr[0] % 2]; b_YC = b_YCs[xp_rr[0] % 2]
                        xp_rr[0] += 1
                        P.dma("sp", XP[:, :], zfm_v[:, zc, HALO + sb0 - 1:HALO + sb0 + SEG + 1], reads=[b_z, b_pad], writes=[b_XP])
                        P.op("dve", lambda h, cidx=cidx, XP=XP, YC=YC: h.tensor_scalar(out=YC[:, :], in0=XP[:, 1:SEG + 1], scalar1=cvw_sb[:, cidx, 1:2], scalar2=None, op0=ALU.mult),
                             reads=[b_XP, b_c], writes=[b_YC])
                        P.op("dve", lambda h, cidx=cidx, XP=XP, YC=YC: h.scalar_tensor_tensor(out=YC[:, :], in0=XP[:, 0:SEG], scalar=cvw_sb[:, cidx, 0:1], in1=YC[:, :],
                             op0=ALU.mult, op1=ALU.add), reads=[b_XP, b_c, b_YC], writes=[b_YC])
                        P.op("dve", lambda h, cidx=cidx, XP=XP, YC=YC: h.scalar_tensor_tensor(out=YC[:, :], in0=XP[:, 2:SEG + 2], scalar=cvw_sb[:, cidx, 2:3], in1=YC[:, :],
                             op0=ALU.mult, op1=ALU.add), reads=[b_XP, b_c, b_YC], writes=[b_YC])
                        if seg > 0:
                            P.op("dve", lambda h, cidx=cidx, XP=XP, YC=YC: h.tensor_scalar(out=TC[:, :], in0=XP[:, 0:1], scalar1=w0s[:, cidx:cidx + 1], scalar2=None, op0=ALU.mult),
                                 reads=[b_XP, b_c], writes=[b_TC])
                            P.op("dve", lambda h, XP=XP, YC=YC: h.tensor_tensor(out=YC[:, 0:1], in0=YC[:, 0:1], in1=TC[:, :], op=ALU.subtract), reads=[b_TC, b_YC], writes=[b_YC])
                        if seg < NSEG - 1:
                            P.op("dve", lambda h, cidx=cidx, XP=XP, YC=YC: h.tensor_scalar(out=TC[:, :], in0=XP[:, SEG + 1:SEG + 2], scalar1=w2s[:, cidx:cidx + 1], scalar2=None, op0=ALU.mult),
                                 reads=[b_XP, b_c], writes=[b_TC])
                            P.op("dve", lambda h, XP=XP, YC=YC: h.tensor_tensor(out=YC[:, SEG - 1:SEG], in0=YC[:, SEG - 1:SEG], in1=TC[:, :], op=ALU.subtract), reads=[b_TC, b_YC], writes=[b_YC])
                        if qk == 0:
                            P.op("act", lambda h, cidx=cidx, XP=XP, YC=YC: h.activation(out=YC[:, :], in_=YC[:, :], func=AF.Silu, bias=cvb_sb[:, cidx:cidx + 1]), reads=[b_YC, b_c], writes=[b_YC])
                            P.op("dve", lambda h, ft=ft, sb0=sb0, XP=XP, YC=YC: h.tensor_scalar(out=QT[:, ft, sb0:sb0 + SEG], in0=YC[:, :], scalar1=1.0 / 16.0, scalar2=None, op0=ALU.mult),
                                 reads=[b_YC], writes=[b_QK])
                        else:
                            P.op("act", lambda h, cidx=cidx, ft=ft, sb0=sb0, XP=XP, YC=YC: h.activation(out=KT[:, ft, sb0:sb0 + SEG], in_=YC[:, :], func=AF.Silu, bias=cvb_sb[:, cidx:cidx + 1]),
                                 reads=[b_YC, b_c], writes=[b_QK])
            for c in range(NCH):
                po, bpo = next_pst()
                for ft in range(2):
                    P.op("pe", lambda h, c=c, ft=ft, po=po: h.transpose(out=po[:, ft, :], in_=KT[:, ft, c * 128:(c + 1) * 128], identity=ident_bf[:, :]),
                         reads=[b_QK, b_c], writes=[bpo])
                P.op("act", lambda h, c=c, po=po: h.copy(out=KTM[:, c, :].rearrange("p (a b) -> p a b", a=2), in_=po[:, 0:2, :]), reads=[bpo], writes=[b_KTM])
            for dr in range(2):
                P.op("dve", lambda h, dr=dr: h.memset(CST[dr][:, :, :], 0.0), writes=[b_CST[dr]])
            def stageA(dr, cc):
                hd = dr * 4 + head
                c = cc if dr == 0 else NCH - 1 - cc
                tk = slice(c * 128, (c + 1) * 128)
                second = cc >= NCH // 2
                i3 = rr[0] % NR; i2 = rr[0] % NA; rr[0] += 1
                r0 = HALO + c * 128
                P.dma("sp", VB[i3][:, :], s_ztm[r0:r0 + 128, 1024 + head * 256:1024 + (head + 1) * 256], reads=[b_z], writes=[b_VB[i3]])
                if second:
                    P.dma("sp", OB[i3][:, :], s_ztm[r0:r0 + 128, 2048 + head * 256:2048 + (head + 1) * 256], reads=[b_z], writes=[b_OB[i3]])
                s1 = SC1[i2]
                P.op("act", lambda h: h.activation(out=s1[:, 0:1], in_=Ug[:, c, hd:hd + 1], func=AF.Exp, bias=NEGMC[:, hd, cc:cc + 1]),
                     reads=[b_scan], writes=[b_SC1[i2]])
                P.op("act", lambda h: h.activation(out=s1[:, 1:2], in_=Fg[:, c, hd:hd + 1], func=AF.Exp, bias=NEGMC[:, hd, cc:cc + 1], scale=-1.0),
                     reads=[b_scan], writes=[b_SC1[i2]])
                vp = VP[i2]
                P.op("dve", lambda h: h.tensor_scalar(out=vp[:, 0:256], in0=VB[i3][:, :], scalar1=s1[:, 0:1], scalar2=None, op0=ALU.mult),
                     reads=[b_VB[i3], b_SC1[i2]], writes=[b_VP[i2]])
                P.op("dve", lambda h: h.tensor_copy(out=vp[:, 256:257], in_=s1[:, 0:1]), reads=[b_SC1[i2]], writes=[b_VP[i2]])
                pS, bpS = next_ps()
                for ft in range(2):
                    P.op("pe", lambda h, ft=ft: h.matmul(pS[:, 0:128], KT[:, ft, tk], QT[:, ft, tk], start=(ft == 0), stop=(ft == 1)),
                         reads=[b_QK], writes=[bpS])
                wt = WT[i2]
                P.op("dve", lambda h: h.tensor_tensor(out=wt[:, :], in0=pS[:, 0:128], in1=cmc_bf[:, dr, :], op=ALU.mult),
                     reads=[bpS, b_c], writes=[b_WT[i2]])
                return dict(dr=dr, cc=cc, c=c, hd=hd, tk=tk, second=second, i3=i3, i2=i2)
            def stageB(ctx):
                dr, cc, c, hd, tk, second, i3, i2 = (ctx[k] for k in ("dr", "cc", "c", "hd", "tk", "second", "i3", "i2"))
                s1 = SC1[i2]; vp = VP[i2]; wt = WT[i2]
                cst = CST[dr]; bcst = b_CST[dr]
                if second:
                    P.dma("sp", HFB[i3][:, :], s_hf[c * 128:(c + 1) * 128, :], reads=[b_hf], writes=[b_HFB[i3]])
                pCs = []
                for kt in range(2):
                    pC, bpC = next_ps()
                    P.op("pe", lambda h, pC=pC, kt=kt: h.matmul(pC[:, 0:257], KTM[:, c, kt * 128:(kt + 1) * 128], vp[:, :], start=True, stop=True),
                         reads=[b_KTM, b_VP[i2]], writes=[bpC])
                    pCs.append((pC, bpC))
                cp = CPB[i2]
                P.op("act", lambda h: h.activation(out=cp[:, :, :], in_=cst[:, :, :], func=AF.Copy, scale=DEC[:, hd, cc:cc + 1]),
                     reads=[bcst, b_scan], writes=[b_CPB[i2]])
                for kt in range(2):
                    pC, bpC = pCs[kt]
                    P.op("dve", lambda h, pC=pC, kt=kt: h.scalar_tensor_tensor(out=cst[:, kt, :], in0=cst[:, kt, :], scalar=DEC[:, hd, cc:cc + 1],
                         in1=pC[:, 0:257], op0=ALU.mult, op1=ALU.add), reads=[bpC, bcst, b_scan], writes=[bcst])
                pO, bpO = next_ps()
                P.op("pe", lambda h: h.matmul(pO[:, 0:257], wt[:, :], vp[:, :], start=True, stop=False), reads=[b_WT[i2], b_VP[i2]], writes=[bpO])
                for kt in range(2):
                    P.op("pe", lambda h, kt=kt: h.matmul(pO[:, 0:257], QT[:, kt, tk], cp[:, kt, :], start=False, stop=(kt == 1)),
                         reads=[b_QK, b_CPB[i2]], writes=[bpO])
                P.op("act", lambda h: h.activation(out=s1[:, 6:7], in_=pO[:, 256:257], func=AF.Abs), reads=[bpO], writes=[b_SC1[i2]])
                P.op("dve", lambda h: h.tensor_tensor(out=s1[:, 2:3], in0=s1[:, 6:7], in1=s1[:, 1:2], op=ALU.max), reads=[b_SC1[i2]], writes=[b_SC1[i2]])
                P.op("dve", lambda h: h.reciprocal(out=s1[:, 3:4], in_=s1[:, 2:3]), reads=[b_SC1[i2]], writes=[b_SC1[i2]])
                hh = HH[i2]
                if not second:
                    P.op("dve", lambda h: h.tensor_scalar(out=hh[:, :], in0=pO[:, 0:256], scalar1=s1[:, 3:4], scalar2=None, op0=ALU.mult),
                         reads=[bpO, b_SC1[i2]], writes=[b_HH[i2]])
                    P.dma("sp", s_hf[c * 128:(c + 1) * 128, :], hh[:, :], reads=[b_HH[i2]], writes=[b_hf])
                else:
                    P.op("dve", lambda h: h.scalar_tensor_tensor(out=hh[:, :], in0=pO[:, 0:256], scalar=s1[:, 3:4], in1=HFB[i3][:, :],
                         op0=ALU.mult, op1=ALU.add), reads=[bpO, b_SC1[i2], b_HFB[i3]], writes=[b_HH[i2]])
                    bs = BST[i2]
                    P.op("dve", lambda h: h.bn_stats(out=bs[:, 0:6], in_=hh[:, :]), reads=[b_HH[i2]], writes=[b_BST[i2]])
                    P.op("dve", lambda h: h.bn_aggr(out=bs[:, 6:8], in_=bs[:, 0:6]), reads=[b_BST[i2]], writes=[b_BST[i2]])
                    P.op("act", lambda h: h.activation(out=s1[:, 4:5], in_=bs[:, 7:8], func=AF.Sqrt, bias=eps_sb[:, 0:1]), reads=[b_BST[i2], b_c], writes=[b_SC1[i2]])
                    P.op("dve", lambda h: h.reciprocal(out=s1[:, 5:6], in_=s1[:, 4:5]), reads=[b_SC1[i2]], writes=[b_SC1[i2]])
                    P.op("dve", lambda h: h.tensor_scalar(out=hh[:, :], in0=hh[:, :], scalar1=bs[:, 6:7], scalar2=s1[:, 5:6], op0=ALU.subtract, op1=ALU.mult),
                         reads=[b_HH[i2], b_BST[i2], b_SC1[i2]], writes=[b_HH[i2]])
                    P.op("dve", lambda h, head=head: h.tensor_tensor(out=hh[:, :], in0=hh[:, :], in1=MN[:, head * 256:(head + 1) * 256], op=ALU.mult),
                         reads=[b_HH[i2], b_MN], writes=[b_HH[i2]])
                    sg = SG[i2]
                    P.op("act", lambda h: h.activation(out=sg[:, :], in_=OB[i3][:, :], func=AF.Sigmoid), reads=[b_OB[i3]], writes=[b_SG[i2]])
                    yb = YB[i2]
                    P.op("dve", lambda h: h.tensor_tensor(out=yb[:, :], in0=hh[:, :], in1=sg[:, :], op=ALU.mult), reads=[b_HH[i2], b_SG[i2]], writes=[b_YB[i2]])
                    yo = YO[i2]
                    po, bpo = next_pst()
                    for ft in range(2):
                        P.op("pe", lambda h, ft=ft: h.transpose(out=po[:, ft, :], in_=yb[:, ft * 128:(ft + 1) * 128], identity=ident_bf[:, :]),
                             reads=[b_YB[i2], b_c], writes=[bpo])
                    P.op("act", lambda h: h.copy(out=yo[:, :, :], in_=po[:, 0:2, :]), reads=[bpo], writes=[b_YO[i2]])
                    P.dma("sp", syT_v[:, 8 + head * 2:8 + head * 2 + 2, c * 128:(c + 1) * 128], yo[:, :, :], reads=[b_YO[i2]], writes=[Buf('yst')])
            pend = [stageA(0, 0), stageA(1, 0)]
            for cc in range(NCH):
                nxt = []
                if cc + 1 < NCH:
                    nxt = [stageA(0, cc + 1), stageA(1, cc + 1)]
                for ctx in pend:
                    stageB(ctx)
                pend = nxt
        P.barrier()
        sc.close()

    if 1 in phases:
        dense_phase(1)
    if 2 in phases:
        attention_phase()
    if 3 in phases:
        scan_setup()
    if 4 in phases:
        mlstm_phase()
    if 5 in phases:
        dense_phase(3)
    P.emit()
    st.close()
    return nc


def _t5_bucket(rel):
    nb = 16
    max_exact = 8
    n = np.abs(rel)
    large = max_exact + (np.log(np.maximum(n, 1) / max_exact) / np.log(1024 / max_exact) * (nb - max_exact)).astype(np.int32)
    large = np.minimum(large, nb - 1)
    return np.where(rel > 0, nb, 0) + np.where(n < max_exact, n, large)


def host_consts(T):
    NCH = T // 128
    NQ = NCH * 8
    NB = NQ // 128
    i = np.arange(128)[:, None]
    j = np.arange(128)[None, :]
    cmask = np.zeros((128, 16, 128), np.float32)
    for bi in range(3):
        cmask[:, bi * 2 + 0, :] = (i >= j)
        cmask[:, bi * 2 + 1, :] = (i <= j)
        cmask[:, 6 + bi * 2 + 0, :] = (i >= j) & (i >= 64)
        cmask[:, 6 + bi * 2 + 1, :] = (i <= j) & (i < 64)
    cmask[:, 12, :] = (i <= j)
    cmask[:, 13, :] = (i >= j)
    cf32 = np.zeros((128, 3, 128), np.float32)
    cf32[:, 0, :] = (i <= j)
    cf32[127, 1, :] = 1.0
    cf32[:, 2, :] = (i == j)
    perm = np.zeros((NB, 128, NQ), np.float32)
    for q in range(NQ):
        c, hd = q // 8, q % 8
        cpp = c if hd < 4 else NCH - 1 - c
        perm[q // 128, q % 128, hd * NCH + cpp] = 1.0
    segm = np.zeros((128, NCH), np.float32)
    segm[:, ::16] = 1.0
    bidx = np.zeros((6, 128, 128), np.int64)
    for bi, d in enumerate((1, 4, 16)):
        bidx[bi * 2 + 0] = _t5_bucket((i - j - 64) * d)
        bidx[bi * 2 + 1] = _t5_bucket((i - j + 64) * d)
    return cmask, cf32, perm, segm, bidx


def host_inputs(inputs, T):
    f = lambda a: np.asarray(a, dtype=np.float32)
    common = {
        "w1in": _lay_ffn_in(f(inputs["ffn1_w_in"])[0]), "w1out": _lay_ffn_out(f(inputs["ffn1_w_out"])[0]),
        "w2in": _lay_ffn_in(f(inputs["ffn2_w_in"])[0]), "w2out": _lay_ffn_out(f(inputs["ffn2_w_out"])[0]),
        "wout": _lay_sq(f(inputs["w_out"])[0], KC), "wgate": _lay_sq(f(inputs["ple_w_gate"])[0], KC),
        "wproj": _lay_sq(f(inputs["ple_w_proj"])[0], 2),
    }
    w_in = f(inputs["w_in"])[0]
    A_, M_ = AW, MW
    fm_cols = np.concatenate([w_in[:, 0:2 * A_], w_in[:, 3 * A_:3 * A_ + 2 * M_]], axis=1)
    tm_cols = np.concatenate([w_in[:, 2 * A_:3 * A_], w_in[:, 3 * A_ + 2 * M_:3 * A_ + 4 * M_]], axis=1)
    t = fm_cols.reshape(KC, 128, 8, 4, 128).transpose(2, 1, 3, 0, 4)
    common["wina"] = np.ascontiguousarray(t).reshape(8, 128, 8192)
    common["winb"] = _lay_w_in_blocks(tm_cols, 512)
    common["wing"] = np.ascontiguousarray(w_in[:, 3 * A_ + 4 * M_:].reshape(KC, 128, NG).transpose(1, 0, 2)).reshape(128, KC * NG)
    common["nrm"] = np.concatenate([_lay_vec(f(inputs["ffn1_norm"])[0]), _lay_vec(f(inputs["mix_norm"])[0]),
                                    _lay_vec(f(inputs["ffn2_norm"])[0]), _lay_vec(f(inputs["ple_norm"])[0]),
                                    _lay_vec(f(inputs["final_norm"]))], axis=1)
    common["anrm"] = _lay_vec(f(inputs["attn_out_norm"])[0])
    common["mnrm"] = np.ascontiguousarray(np.broadcast_to(f(inputs["mlstm_out_norm"])[0][None, :], (128, MW)))
    common["bgr"] = np.ascontiguousarray(np.broadcast_to(f(inputs["b_gates"])[0][None, :], (128, NG)))
    cw = f(inputs["conv_w"])[0]
    common["cvw"] = np.ascontiguousarray(cw.reshape(3, 16, 128).transpose(2, 1, 0))
    common["cvb"] = _lay_vec(f(inputs["conv_b"])[0])
    cmask, cf32, perm, segm, bidx = host_consts(T)
    common["cmask"] = cmask; common["cf32"] = cf32; common["perm"] = perm; common["segm"] = segm
    rt = f(inputs["rel_table"])
    g = rt[bidx]
    common["relb"] = np.ascontiguousarray(g.transpose(3, 1, 0, 2))
    return common


_NC_CACHE = {}


def kernel(**inputs):
    T = 8192
    n = 8
    if T not in _NC_CACHE:
        _NC_CACHE[T] = build(T)
    nc = _NC_CACHE[T]
    common = host_inputs(inputs, T)
    xp = np.asarray(inputs["x_prompt"], np.float32)
    xs = np.asarray(inputs["x_sample"], np.float32)
    pp = np.asarray(inputs["p_prompt"], np.float32)[0]
    psm = np.asarray(inputs["p_sample"], np.float32)[0]
    S_CORES = (0, 4)
    P_CORES = (2, 6)
    toks = []
    for c in range(n):
        if c in S_CORES:
            i = S_CORES.index(c)
            toks.append((xs[i], psm[i], 0.0))
        elif c in P_CORES:
            i = P_CORES.index(c)
            sl = slice(i * 4, i * 4 + 4)
            toks.append((xp[sl].reshape(T, D), pp[sl].reshape(T, PLE), 1.0))
        else:
            toks.append((np.zeros((T, D), np.float32), np.zeros((T, PLE), np.float32), 1.0))
    in_maps = []
    for c in range(n):
        m = dict(common)
        m["xT"] = np.ascontiguousarray(toks[c][0].T)
        m["pT"] = np.ascontiguousarray(toks[c][1].T)
        m["sep"] = np.full((128, 1), toks[c][2], np.float32)
        in_maps.append(m)
    res = run_bass_kernel_spmd(nc, in_maps, core_ids=list(range(n)))
    outs = [np.ascontiguousarray(r["yT"].T) for r in res.results]
    y_sample = np.stack([outs[S_CORES[0]], outs[S_CORES[1]]], axis=0).astype(np.float32)
    y_prompt = np.concatenate([outs[P_CORES[0]].reshape(4, SEG, D), outs[P_CORES[1]].reshape(4, SEG, D)], axis=0).astype(np.float32)
    return (y_prompt, y_sample)
```
